# Optimizing a Trainium2 kernel written in Bass

```python
import jax, jax.numpy as jnp
from jax import lax
import numpy as np

D_MODEL = 1024
BATCH = 32
SEQ = 256
DEPTH = 1
DEC_BATCH = 8
DEC_SEQ = 4096
PAST_LEN = 512

GRID_W = 64
N_HEADS = 8
HEAD_DIM = 64
N_KV_HEADS = 2
ATTN_W = N_HEADS * HEAD_DIM
KV_W = N_KV_HEADS * HEAD_DIM
LRU_W = D_MODEL // 2
LRU_BLOCKS = 8
LRU_BLOCK = LRU_W // LRU_BLOCKS
LRU_C = 8.0
CONV_W = 4
MIX_W = ATTN_W + LRU_W
IN_COLS = ATTN_W + 2 * KV_W + 2 * LRU_W
FFN_DIM = 2816
N_SUB = 3
Q_BLOCK = 128
ROPE_THETA = 10000.0
EPS = 1e-6

kernel_name = "hymba_rglru_gqa_macaron_prefix_dit_step"


def rms_norm(x, g):
    xf = x.astype(jnp.float32)
    y = xf * lax.rsqrt(jnp.mean(xf * xf, axis=-1, keepdims=True) + EPS)
    return (y * g.astype(jnp.float32)).astype(x.dtype)


def modulation(cond, w_mod, b_mod):
    m = jax.nn.silu(cond) @ w_mod + b_mod
    return m.reshape(-1, 1, 3 * N_SUB, D_MODEL)


def swiglu(h, w_in, w_out):
    g, u = jnp.split(h @ w_in, 2, axis=-1)
    return (jax.nn.silu(g) * u) @ w_out


def axial_rope_tables(seq_len):
    rows = seq_len // GRID_W
    r, c = jnp.meshgrid(jnp.arange(rows), jnp.arange(GRID_W), indexing="ij")
    r = r.reshape(-1).astype(jnp.float32)
    c = c.reshape(-1).astype(jnp.float32)
    n_freq = HEAD_DIM // 4
    inv = ROPE_THETA ** (-jnp.arange(n_freq, dtype=jnp.float32) / n_freq)
    ar = r[:, None] * inv
    ac = c[:, None] * inv
    ang = jnp.concatenate([ar, ar, ac, ac], axis=-1)
    return jnp.cos(ang), jnp.sin(ang)


def rotate_half(u):
    u1, u2 = jnp.split(u, 2, axis=-1)
    return jnp.concatenate([-u2, u1], axis=-1)


def apply_axial_rope(x, cos, sin):
    xf = x.astype(jnp.float32)
    xr, xc = jnp.split(xf, 2, axis=-1)
    rot = jnp.concatenate([rotate_half(xr), rotate_half(xc)], axis=-1)
    return (xf * cos[None, :, None, :] + rot * sin[None, :, None, :]).astype(x.dtype)


def blocked_gqa_attention(q, k, v):
    B, S, H, Dh = q.shape
    G = H // N_KV_HEADS
    nblk = S // Q_BLOCK
    qb = q.reshape(B, nblk, Q_BLOCK, N_KV_HEADS, G, Dh).transpose(1, 0, 2, 3, 4, 5)
    scale = HEAD_DIM ** -0.5

    def one_block(qi):
        s = jnp.einsum("bqkgd,btkd->bkgqt", qi, k, preferred_element_type=jnp.float32) * scale
        p = jax.nn.softmax(s, axis=-1).astype(v.dtype)
        return jnp.einsum("bkgqt,btkd->bqkgd", p, v)

    o = lax.map(one_block, qb)
    return o.transpose(1, 0, 2, 3, 4, 5).reshape(B, S, H * Dh)


def centred_depthwise_conv(x, w, b):
    y = lax.conv_general_dilated(
        x, w[:, None, :], window_strides=(1,),
        padding=[(CONV_W // 2, CONV_W - 1 - CONV_W // 2)],
        dimension_numbers=("NWC", "WIO", "NWC"),
        feature_group_count=x.shape[-1])
    return y + b


def block_diag_linear(x, w, b):
    xb = x.reshape(x.shape[:-1] + (LRU_BLOCKS, LRU_BLOCK))
    y = jnp.einsum("bsnd,nde->bsne", xb, w)
    return y.reshape(x.shape) + b


def linear_scan(a, b, h0, reverse):
    if h0 is not None:
        edge = -1 if reverse else 0
        b = b.at[:, edge].add(a[:, edge] * h0)

    def combine(e1, e2):
        a1, b1 = e1
        a2, b2 = e2
        return a1 * a2, a2 * b1 + b2

    _, h = lax.associative_scan(combine, (a, b), axis=1, reverse=reverse)
    return h


def rglru_direction(x, wa, ba, wx, bx, lam, h0, reverse):
    xf = x.astype(jnp.float32)
    r = jax.nn.sigmoid(block_diag_linear(x, wa, ba).astype(jnp.float32))
    i = jax.nn.sigmoid(block_diag_linear(x, wx, bx).astype(jnp.float32))
    log_a = -LRU_C * r * jax.nn.softplus(-lam.astype(jnp.float32))
    a = jnp.exp(log_a)
    b = jnp.sqrt(-jnp.expm1(2.0 * log_a)) * (i * xf)
    h = linear_scan(a, b, None if h0 is None else h0.astype(jnp.float32), reverse)
    h_final = h[:, 0] if reverse else h[:, -1]
    return h, h_final


def bidirectional_rglru(x, wa, ba, wx, bx, lam, h0):
    hf, ff = rglru_direction(x, wa[0], ba[0], wx[0], bx[0], lam[0],
                             None if h0 is None else h0[:, 0], False)
    hb, fb = rglru_direction(x, wa[1], ba[1], wx[1], bx[1], lam[1],
                             None if h0 is None else h0[:, 1], True)
    return (hf + hb).astype(x.dtype), jnp.stack([ff, fb], axis=1).astype(x.dtype)


def parallel_mixer(h, lp, rope, ctx_k, ctx_v, h0):
    (_, _, _, _, w_in, w_out, q_norm, k_norm, conv_w, conv_b,
     lru_wa, lru_ba, lru_wx, lru_bx, lru_lambda) = lp
    B, S, _ = h.shape
    proj = h @ w_in
    q, k, v, xl, gl = jnp.split(
        proj, [ATTN_W, ATTN_W + KV_W, ATTN_W + 2 * KV_W, ATTN_W + 2 * KV_W + LRU_W], axis=-1)
    q = rms_norm(q.reshape(B, S, N_HEADS, HEAD_DIM), q_norm)
    k = rms_norm(k.reshape(B, S, N_KV_HEADS, HEAD_DIM), k_norm)
    v = v.reshape(B, S, N_KV_HEADS, HEAD_DIM)
    if rope is not None:
        q = apply_axial_rope(q, *rope)
        k_att = apply_axial_rope(k, *rope)
    else:
        k_att = k
    if ctx_k is not None:
        k_all = jnp.concatenate([k_att, ctx_k], axis=1)
        v_all = jnp.concatenate([v, ctx_v], axis=1)
    else:
        k_all, v_all = k_att, v
    attn_out = blocked_gqa_attention(q, k_all, v_all)
    xc = centred_depthwise_conv(xl, conv_w, conv_b)
    y_lru, h_final = bidirectional_rglru(xc, lru_wa, lru_ba, lru_wx, lru_bx, lru_lambda, h0)
    lru_out = y_lru * jax.nn.gelu(gl)
    out = jnp.concatenate([attn_out, lru_out], axis=-1) @ w_out
    return out, k, v, h_final


def trunk_layer(x, m, lp, rope, ctx_k, ctx_v, h0):
    norm_pre, norm_post, ffn_w_in, ffn_w_out = lp[0], lp[1], lp[2], lp[3]

    def pre(x, j):
        return rms_norm(x, norm_pre[j]) * (1 + m[:, :, 3 * j + 1]) + m[:, :, 3 * j]

    def post(x, out, j, res_w):
        return x + res_w * m[:, :, 3 * j + 2] * rms_norm(out, norm_post[j])

    x = post(x, swiglu(pre(x, 0), ffn_w_in[0], ffn_w_out[0]), 0, 0.5)
    mix, k, v, h_final = parallel_mixer(pre(x, 1), lp, rope, ctx_k, ctx_v, h0)
    x = post(x, mix, 1, 1.0)
    x = post(x, swiglu(pre(x, 2), ffn_w_in[1], ffn_w_out[1]), 2, 0.5)
    return x, k, v, h_final


def take_layer(l, *arrays):
    return tuple(a[l] for a in arrays)


def setup_inputs(seed: int = 0) -> dict:
    key = jax.random.key(seed)
    ks = jax.random.split(key, 24)
    f32 = jnp.float32

    def nrm(k, shape, scale):
        return jax.random.normal(k, shape, f32) * scale

    u = jax.random.uniform(ks[23], (DEPTH, 2, LRU_W), f32, 0.9, 0.999)
    a = u ** (1.0 / LRU_C)
    return {
        "x_prompt": nrm(ks[0], (BATCH, SEQ, D_MODEL), 1.0),
        "x_sample": nrm(ks[1], (DEC_BATCH, DEC_SEQ, D_MODEL), 1.0),
        "c": nrm(ks[2], (DEC_BATCH, D_MODEL), 1.0),
        "cache_k": nrm(ks[3], (DEC_BATCH, DEPTH, PAST_LEN, N_KV_HEADS, HEAD_DIM), 1.0),
        "cache_v": nrm(ks[4], (DEC_BATCH, DEPTH, PAST_LEN, N_KV_HEADS, HEAD_DIM), 1.0),
        "state_lru": nrm(ks[5], (DEC_BATCH, DEPTH, 2, LRU_W), 0.5),
        "c_ctx": nrm(ks[6], (D_MODEL,), 1.0),
        "w_mod": nrm(ks[7], (DEPTH, D_MODEL, 3 * N_SUB * D_MODEL), 0.5 * D_MODEL ** -0.5),
        "b_mod": nrm(ks[8], (DEPTH, 3 * N_SUB * D_MODEL), 0.02),
        "norm_pre": 1.0 + nrm(ks[9], (DEPTH, N_SUB, D_MODEL), 0.1),
        "norm_post": 1.0 + nrm(ks[10], (DEPTH, N_SUB, D_MODEL), 0.1),
        "ffn_w_in": nrm(ks[11], (DEPTH, 2, D_MODEL, 2 * FFN_DIM), D_MODEL ** -0.5),
        "ffn_w_out": nrm(ks[12], (DEPTH, 2, FFN_DIM, D_MODEL), FFN_DIM ** -0.5),
        "w_in": nrm(ks[13], (DEPTH, D_MODEL, IN_COLS), D_MODEL ** -0.5),
        "w_out": nrm(ks[14], (DEPTH, MIX_W, D_MODEL), MIX_W ** -0.5),
        "q_norm": 1.0 + nrm(ks[15], (DEPTH, HEAD_DIM), 0.1),
        "k_norm": 1.0 + nrm(ks[16], (DEPTH, HEAD_DIM), 0.1),
        "conv_w": nrm(ks[17], (DEPTH, CONV_W, LRU_W), CONV_W ** -0.5),
        "conv_b": nrm(ks[18], (DEPTH, LRU_W), 0.02),
        "lru_wa": nrm(ks[19], (DEPTH, 2, LRU_BLOCKS, LRU_BLOCK, LRU_BLOCK), LRU_BLOCK ** -0.5),
        "lru_ba": nrm(ks[20], (DEPTH, 2, LRU_W), 0.02),
        "lru_wx": nrm(ks[21], (DEPTH, 2, LRU_BLOCKS, LRU_BLOCK, LRU_BLOCK), LRU_BLOCK ** -0.5),
        "lru_bx": nrm(ks[22], (DEPTH, 2, LRU_W), 0.02),
        "lru_lambda": jnp.log(a) - jnp.log1p(-a),
    }


def reference(x_prompt, x_sample, c, cache_k, cache_v, state_lru, c_ctx, w_mod, b_mod,
              norm_pre, norm_post, ffn_w_in, ffn_w_out, w_in, w_out, q_norm, k_norm,
              conv_w, conv_b, lru_wa, lru_ba, lru_wx, lru_bx, lru_lambda):
    layer_arrays = (norm_pre, norm_post, ffn_w_in, ffn_w_out, w_in, w_out, q_norm, k_norm,
                    conv_w, conv_b, lru_wa, lru_ba, lru_wx, lru_bx, lru_lambda)
    y_prompt = x_prompt
    ks, vs, hs = [], [], []
    for l in range(DEPTH):
        lp = take_layer(l, *layer_arrays)
        m_ctx = modulation(c_ctx, w_mod[l], b_mod[l])
        y_prompt, k_l, v_l, h_l = trunk_layer(y_prompt, m_ctx, lp, None, None, None, None)
        ks.append(k_l)
        vs.append(v_l)
        hs.append(h_l)
    new_cache_k = jnp.stack(ks, axis=1)
    new_cache_v = jnp.stack(vs, axis=1)
    new_state_lru = jnp.stack(hs, axis=1)
    rope = axial_rope_tables(x_sample.shape[1])
    y_sample = x_sample
    for l in range(DEPTH):
        lp = take_layer(l, *layer_arrays)
        m_lat = modulation(c, w_mod[l], b_mod[l])
        y_sample, _, _, _ = trunk_layer(y_sample, m_lat, lp, rope,
                                        cache_k[:, l], cache_v[:, l], state_lru[:, l])
    return (y_prompt, y_sample, new_cache_k, new_cache_v, new_state_lru)
```

```python
import numpy as np
from contextlib import ExitStack
import concourse.bass as bass
import concourse.mybir as mybir
from concourse.bass_utils import run_bass_kernel_spmd
from concourse.ap import AP

F32 = mybir.dt.float32
BF16 = mybir.dt.bfloat16
ALU = mybir.AluOpType
AF = mybir.ActivationFunctionType
AX = mybir.AxisListType

ENGS = ("pe", "act", "dve", "pool", "sp")

D = 1024
TT = 256
NT = 20
NTS = 16
LS = 4096
LP = 256
PAST = 512
FF = 2816
NFC = 22
EPS = 1e-6
NCORES = 8

C_C2 = 0
C_BMOD = C_C2 + 16
C_NPRE = C_BMOD + 72
C_NPOST = C_NPRE + 24
C_CONVW = C_NPOST + 24
C_CONVB = C_CONVW + 16
C_BA = C_CONVB + 4
C_BX = C_BA + 8
C_LAM = C_BX + 8
C_H0 = C_LAM + 8
NPAR = C_H0 + 8


class Prog:
    def __init__(self, nc, stack):
        self.nc = nc
        self.stack = stack
        self.q = {e: [] for e in ENGS}
        self.cnt = {e: 0 for e in ENGS}
        self.sem = {}
        for e in ("pe", "act", "dve", "pool"):
            self.sem[e] = stack.enter_context(nc.semaphore("s_" + e))
        self.seen = {e: {} for e in ENGS}
        self.last_w = {}
        self.readers = {}
        self.chan = {}
        self.nsem = 4
        self.ninstr = 0

    def _need(self, eng, tok, out):
        if tok is None:
            return
        key, val, semh, prod = tok
        if prod == "pe" and eng == "pe":
            return
        if self.seen[eng].get(key, 0) >= val:
            return
        cur = out.get(key)
        if cur is None or cur[0] < val:
            out[key] = (val, semh)

    def _deps(self, eng, reads, writes):
        out = {}
        for r in reads:
            self._need(eng, self.last_w.get(r), out)
        for w in writes:
            self._need(eng, self.last_w.get(w), out)
            rd = self.readers.get(w)
            if rd:
                for t in rd.values():
                    self._need(eng, t, out)
        for key, (val, semh) in out.items():
            self.seen[eng][key] = val
            self.q[eng].append(("wait", semh, val))

    def _commit(self, tok, reads, writes):
        for r in reads:
            d = self.readers.setdefault(r, {})
            cur = d.get(tok[0])
            if cur is None or cur[1] < tok[1]:
                d[tok[0]] = tok
        for w in writes:
            self.last_w[w] = tok
            self.readers[w] = {}

    def op(self, eng, fn, reads=(), writes=()):
        self._deps(eng, reads, writes)
        self.cnt[eng] += 1
        semh = self.sem[eng]
        self.q[eng].append(("op", fn, semh, 1))
        tok = ("e_" + eng, self.cnt[eng], semh, eng)
        self._commit(tok, reads, writes)
        return tok

    def dma(self, eng, ch, fn, reads=(), writes=()):
        if ch not in self.chan:
            s = self.stack.enter_context(self.nc.semaphore("c_" + ch))
            self.chan[ch] = [s, 0]
            self.nsem += 1
        self._deps(eng, reads, writes)
        c = self.chan[ch]
        c[1] += 16
        self.q[eng].append(("op", fn, c[0], 16))
        tok = ("c_" + ch, c[1], c[0], "dma")
        self._commit(tok, reads, writes)
        return tok

    def barrier(self):
        for eng in ENGS:
            out = {}
            for e in ("pe", "act", "dve", "pool"):
                if e != eng and self.cnt[e] > 0:
                    self._need(eng, ("e_" + e, self.cnt[e], self.sem[e], "x"), out)
            for ch, (s, v) in self.chan.items():
                if v > 0:
                    self._need(eng, ("c_" + ch, v, s, "dma"), out)
            for key, (val, semh) in out.items():
                self.seen[eng][key] = val
                self.q[eng].append(("wait", semh, val))

    def emit(self):
        nc = self.nc
        q = self.q
        self.ninstr += sum(len(v) for v in q.values())

        def run(engine, items):
            for it in items:
                if it[0] == "wait":
                    engine.wait_ge(it[1], it[2])
                else:
                    ins = it[1](engine)
                    ins.then_inc(it[2], it[3])

        with nc.Block() as block:
            @block.tensor
            def _(e):
                run(e, q["pe"])

            @block.scalar
            def _(e):
                run(e, q["act"])

            @block.vector
            def _(e):
                run(e, q["dve"])

            @block.gpsimd
            def _(e):
                run(e, q["pool"])

            @block.sync
            def _(e):
                run(e, q["sp"])
        self.q = {e: [] for e in ENGS}


def cust(base, dims):
    return AP(base.tensor, base.offset, [list(base.ap[0])] + [list(d) for d in dims])


class Builder:
    def __init__(self, debug=False, stop_after=None):
        self.debug = debug
        self.stop_after = stop_after
        self.nc = bass.Bass("TRN2", target_bir_lowering=False)
        nc = self.nc
        di = lambda n, s: nc.dram_tensor(n, s, F32, kind="ExternalInput").ap()
        do = lambda n, s: nc.dram_tensor(n, s, F32, kind="ExternalOutput").ap()
        ds = lambda n, s, dt=F32: nc.dram_tensor(n, s, dt, kind="Internal").ap()
        self.x_all = di("x_all", [NT * TT, D])
        self.params = di("params", [128, NPAR])
        self.w_mod = di("w_mod", [D, 9 * D])
        self.ffn_w_in = di("ffn_w_in", [2, D, 2 * FF])
        self.ffn_w_out = di("ffn_w_out", [2, FF, D])
        self.w_in = di("w_in", [D, 1792])
        self.w_out = di("w_out", [D, D])
        self.gvec = di("gvec", [640])
        self.rope = di("rope", [LS, 128])
        self.cache_k = di("cache_k", [PAST, 128])
        self.cache_v = di("cache_v", [PAST, 128])
        self.bd = di("bd", [128, 16 * 128])
        self.y_all = do("y_all", [NT * TT, D])
        self.nck = do("nck", [4 * LP, 128])
        self.ncv = do("ncv", [4 * LP, 128])
        self.nst = do("nst", [32, 128])
        self.x1s = ds("x1s", [NT, 128, 8 * TT])
        self.x2s = ds("x2s", [NT, 128, 8 * TT])
        self.xls = ds("xls", [4, 128, NT * TT])
        self.ggs = ds("ggs", [4, 128, NT * TT])
        if debug:
            self.dbg_mods = do("dbg_mods", [128, 144])
            self.dbg_x1 = do("dbg_x1", [NT, 128, 8 * TT])
            self.dbg_x2 = do("dbg_x2", [NT, 128, 8 * TT])
            self.dbg_attn = do("dbg_attn", [128, 4 * NT * TT])
            self.dbg_lru = do("dbg_lru", [128, 4 * NT * TT])

    def T(self, st, name, shape, dt):
        return st.enter_context(self.nc.sbuf_tensor(name, shape, dt))

    def PS(self, st, name, shape, dt):
        return st.enter_context(self.nc.psum_tensor(name, shape, dt))

    def pe_group(self, mms, reads, writes):
        def fn(e, mms=mms):
            ins = None
            for (o, l, r, s, t) in mms:
                ins = e.matmul(o, lhsT=l, rhs=r, start=s, stop=t)
            return ins
        return self.P.op("pe", fn, reads, writes)

    def pe_transposes(self, trs, reads, writes):
        def fn(e, trs=trs):
            ins = None
            for (o, i, idn) in trs:
                ins = e.transpose(o, i, idn)
            return ins
        return self.P.op("pe", fn, reads, writes)

    def mod(self, k, j, dc, cd):
        c = ((k * 3 + j) * 8 + dc) * 2 + cd
        return self.mods[:, c:c + 1]

    def build(self):
        nc = self.nc
        with ExitStack() as top:
            self.P = Prog(nc, top)
            P = self.P
            self.par = self.T(top, "par", [128, NPAR], F32)
            self.mods = self.T(top, "mods", [128, 144], F32)
            self.identf = self.T(top, "identf", [128, 128], F32)
            self.identb = self.T(top, "identb", [128, 128], BF16)
            self.onesb = self.T(top, "onesb", [128, 128], BF16)
            P.dma("sp", "par", lambda e: e.dma_start(out=self.par[:], in_=self.params), writes=["par"])
            P.op("pool", lambda e: e.memset(self.identf[:], 0.0), writes=["identf"])
            P.op("pool", lambda e: e.affine_select(out=self.identf[:], in_=self.identf[:], pattern=[[-1, 128]],
                                                   compare_op=ALU.not_equal, fill=1.0, base=0, channel_multiplier=1),
                 reads=["identf"], writes=["identf"])
            P.op("pool", lambda e: e.tensor_copy(out=self.identb[:], in_=self.identf[:]), reads=["identf"], writes=["identb"])
            P.op("pool", lambda e: e.memset(self.onesb[:], 1.0), writes=["onesb"])
            self.epsc = self.T(top, "epsc", [128, 1], F32)
            P.op("pool", lambda e: e.memset(self.epsc[:], EPS), writes=["epsc"])

            self.phase_mod()
            if self.stop_after == "mod":
                return self.finish()
            self.phase_ffn(0)
            if self.stop_after == "ffn1":
                return self.finish()
            self.phase_mixer(0)
            if self.stop_after == "mix0":
                return self.finish()
            with ExitStack() as pre:
                W1n = self.T(pre, "F1_W1", [128, 8, 2 * FF], BF16)
                w1v = self.ffn_w_in[1].rearrange("(k p) f -> p k f", p=128)

                def prefetch():
                    for cb in range(4):
                        for hf in range(2):
                            c0 = hf * FF + cb * 704
                            P.dma("pool", f"w1_{cb}_{hf}", lambda e, c0=c0: e.dma_start(out=W1n[:, :, c0:c0 + 704], in_=w1v[:, :, c0:c0 + 704]),
                                  writes=[f"F1_W1_{cb}_{hf}"])
                self.after_win_hook = prefetch
                self.phase_mixer(1)
                self.after_win_hook = None
                if self.stop_after == "mix1":
                    return self.finish()
                self.phase_ffn(1, W1_pre=W1n)
            return self.finish()

    def finish(self):
        self.P.barrier()
        self.P.emit()
        return self.nc

    def phase_mod(self):
        P = self.P
        par = self.par
        with ExitStack() as ph:
            wm = [self.T(ph, f"wm{i}", [128, 8, 512], F32) for i in range(2)]
            sc = self.T(ph, "sc", [128, 16], F32)
            mT = self.T(ph, "mT", [128, 144], F32)
            tmp = self.T(ph, "mtmp", [128, 16], F32)
            pm = self.PS(ph, "pm", [128, 512], F32)
            P.op("act", lambda e: e.activation(out=sc[:], in_=par[:, C_C2:C_C2 + 16], func=AF.Silu),
                 reads=["par"], writes=["sc"])
            wmv = self.w_mod.rearrange("(k p) f -> p k f", p=128)
            for hb in range(18):
                b = hb % 2
                P.dma("sp", f"wm{b}", lambda e, b=b, hb=hb: e.dma_start(out=wm[b][:], in_=wmv[:, :, hb * 512:(hb + 1) * 512]),
                      writes=[f"wm{b}"])
                for q in range(4):
                    g = hb * 4 + q
                    mms = [(pm[:, g * 2:g * 2 + 2], wm[b][:, kc, q * 128:(q + 1) * 128], sc[:, kc * 2:kc * 2 + 2],
                            kc == 0, kc == 7) for kc in range(8)]
                    self.pe_group(mms, reads=[f"wm{b}", "sc"], writes=["pm"])
            bm = cust(par[:, C_BMOD:C_BMOD + 72], [[1, 72], [0, 2]])
            P.op("dve", lambda e: e.tensor_tensor(out=mT[:].rearrange("p (a b) -> p a b", b=2),
                                                  in0=pm[:, 0:144].rearrange("p (a b) -> p a b", b=2), in1=bm, op=ALU.add),
                 reads=["pm", "par"], writes=["mT"])
            mods = self.mods
            for j in range(3):
                rw = 1.0 if j == 1 else 0.5
                P.op("dve", lambda e, j=j: e.tensor_scalar(out=tmp[:], in0=mT[:, (3 * j + 1) * 16:(3 * j + 2) * 16],
                                                           scalar1=1.0, scalar2=None, op0=ALU.add),
                     reads=["mT"], writes=["mtmp"])
                npre = cust(par[:, C_NPRE + j * 8:C_NPRE + j * 8 + 8], [[1, 8], [0, 2]])
                P.op("dve", lambda e, j=j, npre=npre: e.tensor_tensor(
                    out=mods[:, (0 * 3 + j) * 16:(0 * 3 + j) * 16 + 16].rearrange("p (a b) -> p a b", b=2),
                    in0=tmp[:].rearrange("p (a b) -> p a b", b=2), in1=npre, op=ALU.mult),
                    reads=["mtmp", "par"], writes=["mods"])
                P.op("dve", lambda e, j=j: e.tensor_copy(out=mods[:, (1 * 3 + j) * 16:(1 * 3 + j) * 16 + 16],
                                                         in_=mT[:, (3 * j) * 16:(3 * j) * 16 + 16]),
                     reads=["mT"], writes=["mods"])
                npost = cust(par[:, C_NPOST + j * 8:C_NPOST + j * 8 + 8], [[1, 8], [0, 2]])
                P.op("dve", lambda e, j=j, npost=npost, rw=rw: e.scalar_tensor_tensor(
                    out=mods[:, (2 * 3 + j) * 16:(2 * 3 + j) * 16 + 16].rearrange("p (a b) -> p a b", b=2),
                    in0=mT[:, (3 * j + 2) * 16:(3 * j + 3) * 16].rearrange("p (a b) -> p a b", b=2),
                    scalar=rw, in1=npost, op0=ALU.mult, op1=ALU.mult),
                    reads=["mT", "par"], writes=["mods"])
            if self.debug:
                P.dma("sp", "dbg", lambda e: e.dma_start(out=self.dbg_mods, in_=mods[:]), reads=["mods"])
            P.barrier()
            P.emit()

    def prenorm(self, xT, hT, j, cd, bufs, tag, xres=None):
        P = self.P
        sq, st, t1, rstd, tmp = bufs["sq"], bufs["st"], bufs["t1"], bufs["rstd"], bufs["tmp"]
        xr, hr = (xres or tag + "xT"), tag + "hT"
        for kc in range(8):
            b = kc % 2
            P.op("act", lambda e, kc=kc, b=b: e.activation(out=sq[b][:], in_=xT[:, kc, :], func=AF.Square),
                 reads=[xr], writes=[f"sq{b}"])
            self.pe_group([(st[:, 0:TT], self.onesb[:], sq[b][:], kc == 0, kc == 7)], reads=[f"sq{b}", "onesb"], writes=["st"])
        P.op("act", lambda e: e.activation(out=t1[:], in_=st[:, 0:TT], func=AF.Ln, scale=1.0 / D, bias=self.epsc[:, 0:1]),
             reads=["st", "epsc"], writes=["t1"])
        P.op("act", lambda e: e.activation(out=rstd[:], in_=t1[:], func=AF.Exp, scale=-0.5),
             reads=["t1"], writes=["rstd"])
        for kc in range(8):
            b = kc % 2
            P.op("dve", lambda e, kc=kc, b=b: e.tensor_tensor(out=tmp[b][:], in0=xT[:, kc, :], in1=rstd[:], op=ALU.mult),
                 reads=[xr, "rstd"], writes=[f"tmp{b}"])
            P.op("act", lambda e, kc=kc, b=b: e.activation(out=hT[:, kc, :], in_=tmp[b][:], func=AF.Identity,
                                                           scale=self.mod(0, j, kc, cd), bias=self.mod(1, j, kc, cd)),
                 reads=[f"tmp{b}", "mods"], writes=[hr])

    def post(self, xT, oT, j, cd, bufs, tag):
        P = self.P
        st2, t1, rstd, tmp = bufs["st2"], bufs["t1"], bufs["rstd"], bufs["tmpf"]
        xr = tag + "xT"
        P.op("act", lambda e: e.activation(out=t1[:], in_=st2[:, 0:TT], func=AF.Ln, scale=1.0 / D, bias=self.epsc[:, 0:1]),
             reads=["st2", "epsc"], writes=["t1"])
        P.op("act", lambda e: e.activation(out=rstd[:], in_=t1[:], func=AF.Exp, scale=-0.5),
             reads=["t1"], writes=["rstd"])
        for dc in range(8):
            b = dc % 2
            P.op("dve", lambda e, dc=dc, b=b: e.scalar_tensor_tensor(out=tmp[b][:], in0=oT[:, dc, :], scalar=self.mod(2, j, dc, cd),
                                                                     in1=rstd[:], op0=ALU.mult, op1=ALU.mult),
                 reads=[f"oT{dc}", "rstd", "mods"], writes=[f"tmpf{b}"])
            P.op("pool", lambda e, dc=dc, b=b: e.tensor_tensor(out=xT[:, dc, :], in0=xT[:, dc, :], in1=tmp[b][:], op=ALU.add),
                 reads=[xr, f"tmpf{b}"], writes=[xr])

    def out_groups(self, groups_fn, oT, bufs):
        P = self.P
        po, sq, st2 = bufs["po"], bufs["sq"], bufs["st2"]
        def stat(dc):
            b = dc % 2
            self.pe_group([(st2[:, 0:TT], self.onesb[:], sq[b][:], dc == 0, dc == 7)], reads=[f"sq{b}", "onesb"], writes=["st2"])
        for dc in range(8):
            b = dc % 2
            mms, reads = groups_fn(dc, po[b][:, 0:TT])
            self.pe_group(mms, reads=reads, writes=[f"po{b}"])
            if dc >= 1:
                stat(dc - 1)
            P.op("dve", lambda e, dc=dc, b=b: e.tensor_copy(out=oT[:, dc, :], in_=po[b][:, 0:TT]), reads=[f"po{b}"], writes=[f"oT{dc}"])
            P.op("act", lambda e, b=b, dc=dc: e.activation(out=sq[b][:], in_=oT[:, dc, :], func=AF.Square),
                 reads=[f"oT{dc}"], writes=[f"sq{b}"])
        stat(7)

    def phase_ffn(self, which, W1_pre=None):
        P = self.P
        j = 0 if which == 0 else 2
        fp = f"F{which}_"
        with ExitStack() as ph:
            if W1_pre is not None:
                W1g, W1u = W1_pre[:, :, 0:FF], W1_pre[:, :, FF:2 * FF]
            else:
                W1 = self.T(ph, fp + "W1", [128, 8, 2 * FF], BF16)
                W1g, W1u = W1[:, :, 0:FF], W1[:, :, FF:2 * FF]
            W2 = self.T(ph, fp + "W2", [128, NFC, D], BF16)
            xT = [self.T(ph, fp + f"xT{i}", [128, 8, TT], F32) for i in range(3)]
            hT = [self.T(ph, fp + f"hT{i}", [128, 8, TT], BF16) for i in range(2)]
            actT = self.T(ph, fp + "actT", [128, NFC, TT], BF16)
            oT = self.T(ph, fp + "oT", [128, 8, TT], F32)
            sg = [self.T(ph, fp + f"sg{i}", [128, TT], F32) for i in range(2)]
            tok = [self.T(ph, fp + f"tok{i}", [128, D], F32) for i in range(2)]
            sq8 = self.T(ph, fp + "sq8", [128, 8, TT], BF16)
            sqp = [self.T(ph, fp + f"sqp{i}", [128, TT], BF16) for i in range(2)]
            t1a = self.T(ph, fp + "t1a", [128, TT], F32)
            rsa = self.T(ph, fp + "rsa", [128, TT], F32)
            t1b = t1a
            t1c = t1a
            rsb = self.T(ph, fp + "rsb", [128, TT], F32)
            tmp = [self.T(ph, fp + f"tmp{i}", [128, TT], F32) for i in range(2)]
            tmpf = tmp
            tmpn = [self.T(ph, fp + f"tmpn{i}", [128, TT], F32) for i in range(2)]
            st = self.PS(ph, fp + "st", [128, 512], F32)
            st2 = self.PS(ph, fp + "st2", [128, 512], F32)
            po = [self.PS(ph, fp + f"po{i}", [128, 512], F32) for i in range(2)]
            pgu = [self.PS(ph, fp + f"pgu{i}", [128, 512], F32) for i in range(2)]
            ptr = [self.PS(ph, fp + f"ptr{i}", [128, 512], F32) for i in range(2)]
            w1v = self.ffn_w_in[which].rearrange("(k p) f -> p k f", p=128)
            CB = 704
            for cb in range(4):
                for hf in range(2):
                    if W1_pre is not None:
                        continue
                    dstw = W1g if hf == 0 else W1u
                    c0 = cb * CB
                    P.dma("pool", f"w1_{cb}_{hf}", lambda e, c0=c0, hf=hf, dstw=dstw: e.dma_start(
                        out=dstw[:, :, c0:c0 + CB], in_=w1v[:, :, hf * FF + c0:hf * FF + c0 + CB]),
                        writes=[fp + f"W1_{cb}_{hf}"])
            w2v = self.ffn_w_out[which].rearrange("(f p) d -> p f d", p=128)
            for q in range(2):
                P.dma("pool", f"w2_{q}", lambda e, q=q: e.dma_start(out=W2[:, q * 11:(q + 1) * 11, :], in_=w2v[:, q * 11:(q + 1) * 11, :]),
                      writes=[fp + f"W2_{q}"])
            def w1res_for(fc):
                cbs = sorted({(fc * 128) // CB, (fc * 128 + 127) // CB})
                return [fp + f"W1_{cb}_{hf}" for cb in cbs for hf in range(2)]
            w2res = [fp + "W2_0", fp + "W2_1"]
            src = self.x2s
            ntl = getattr(self, "ntl", NT)
            xr = lambda t: fp + f"xT{t % 3}"
            hr = lambda t: fp + f"hT{t % 2}"
            cdof = lambda t: 0 if t < NTS else 1

            def load(t):
                if t >= ntl:
                    return
                if which == 0:
                    for sub in range(2):
                        r0 = t * TT + sub * 128
                        P.dma("sp", f"tok{sub}", lambda e, sub=sub, r0=r0: e.dma_start(out=tok[sub][:], in_=self.x_all[r0:r0 + 128, :]),
                              writes=[fp + f"tok{sub}"])
                else:
                    s3 = t % 3
                    P.dma("sp", f"xin{s3}", lambda e, s3=s3, t=t: e.dma_start(out=xT[s3][:].rearrange("p a b -> p (a b)"), in_=src[t]),
                          reads=[f"x2s{t}"], writes=[xr(t)])

            def transp_in(t):
                if t >= ntl or which != 0:
                    return
                s3 = t % 3
                for sub in range(2):
                    for half in range(2):
                        pb = half
                        trs = [(ptr[pb][:, jj * 128:(jj + 1) * 128], tok[sub][:, (half * 4 + jj) * 128:(half * 4 + jj + 1) * 128],
                                self.identf[:]) for jj in range(4)]
                        self.pe_transposes(trs, reads=[fp + f"tok{sub}", "identf"], writes=[fp + f"ptr{pb}"])
                        P.op("dve", lambda e, s3=s3, half=half, sub=sub, pb=pb: e.tensor_copy(
                            out=xT[s3][:, half * 4:(half + 1) * 4, sub * 128:(sub + 1) * 128],
                            in_=ptr[pb][:].rearrange("p (a b) -> p a b", b=128)),
                            reads=[fp + f"ptr{pb}"], writes=[xr(t)])

            def squares(t):
                if t >= ntl:
                    return
                s3 = t % 3
                for kc in range(8):
                    P.op("act", lambda e, kc=kc, s3=s3: e.activation(out=sq8[:, kc, :], in_=xT[s3][:, kc, :], func=AF.Square),
                         reads=[xr(t)], writes=[fp + f"sq8_{kc}"])

            def stat_mm(t):
                if t >= ntl:
                    return
                for kc in range(8):
                    self.pe_group([(st[:, 0:TT], self.onesb[:], sq8[:, kc, :], kc == 0, kc == 7)],
                                  reads=[fp + f"sq8_{kc}", "onesb"], writes=[fp + "st"])

            def norm_apply(t, step=None):
                if t >= ntl:
                    return
                s3, s2, cd = t % 3, t % 2, cdof(t)
                if step is None or step == 0:
                    P.op("act", lambda e: e.activation(out=t1c[:], in_=st[:, 0:TT], func=AF.Ln, scale=1.0 / D, bias=self.epsc[:, 0:1]),
                         reads=[fp + "st", "epsc"], writes=[fp + "t1a"])
                    P.op("act", lambda e: e.activation(out=rsa[:], in_=t1c[:], func=AF.Exp, scale=-0.5), reads=[fp + "t1a"], writes=[fp + "rsa"])
                kcs = range(8) if step is None else (range(0) if step == 0 else range((step - 1) * 2, step * 2))
                for kc in kcs:
                    b = kc % 2
                    P.op("dve", lambda e, kc=kc, b=b, s3=s3: e.tensor_tensor(out=tmpn[b][:], in0=xT[s3][:, kc, :], in1=rsa[:], op=ALU.mult),
                         reads=[xr(t), fp + "rsa"], writes=[fp + f"tmpn{b}"])
                    P.op("act", lambda e, kc=kc, b=b, s2=s2, cd=cd: e.activation(out=hT[s2][:, kc, :], in_=tmpn[b][:], func=AF.Identity,
                                                                              scale=self.mod(0, j, kc, cd), bias=self.mod(1, j, kc, cd)),
                         reads=[fp + f"tmpn{b}", "mods"], writes=[hr(t)])

            def mm1(t, f0=0, f1=NFC, hook=None):
                s2 = t % 2
                for fc in range(f0, f1):
                    b = fc % 2
                    mms = [(pgu[b][:, 0:TT], W1g[:, kc, fc * 128:(fc + 1) * 128], hT[s2][:, kc, :], kc == 0, kc == 7) for kc in range(8)]
                    mms += [(pgu[b][:, TT:2 * TT], W1u[:, kc, fc * 128:(fc + 1) * 128], hT[s2][:, kc, :], kc == 0, kc == 7)
                            for kc in range(8)]
                    self.pe_group(mms, reads=w1res_for(fc) + [hr(t)], writes=[fp + f"pgu{b}"])
                    P.op("act", lambda e, b=b: e.activation(out=sg[b][:], in_=pgu[b][:, 0:TT], func=AF.Silu),
                         reads=[fp + f"pgu{b}"], writes=[fp + f"sg{b}"])
                    P.op("dve", lambda e, b=b, fc=fc: e.tensor_tensor(out=actT[:, fc, :], in0=sg[b][:], in1=pgu[b][:, TT:2 * TT], op=ALU.mult),
                         reads=[fp + f"sg{b}", fp + f"pgu{b}"], writes=[fp + f"actT{fc}"])
                    if hook is not None:
                        hook(fc)

            actres = [fp + f"actT{fc}" for fc in range(NFC)]

            def mm2(t, hook=None):
                def stat(dc):
                    b = dc % 2
                    self.pe_group([(st2[:, 0:TT], self.onesb[:], sqp[b][:], dc == 0, dc == 7)], reads=[fp + f"sqp{b}", "onesb"], writes=[fp + "st2"])
                for dc in range(8):
                    b = dc % 2
                    mms = [(po[b][:, 0:TT], W2[:, fc, dc * 128:(dc + 1) * 128], actT[:, fc, :], fc == 0, fc == NFC - 1) for fc in range(NFC)]
                    self.pe_group(mms, reads=w2res + actres, writes=[fp + f"po{b}"])
                    if dc >= 1:
                        stat(dc - 1)
                    P.op("dve", lambda e, dc=dc, b=b: e.tensor_copy(out=oT[:, dc, :], in_=po[b][:, 0:TT]),
                         reads=[fp + f"po{b}"], writes=[fp + f"oT{dc}"])
                    P.op("act", lambda e, dc=dc, b=b: e.activation(out=sqp[b][:], in_=oT[:, dc, :], func=AF.Square),
                         reads=[fp + f"oT{dc}"], writes=[fp + f"sqp{b}"])
                    if hook is not None:
                        hook(dc)
                stat(7)

            def post(t, step=None):
                s3, cd = t % 3, cdof(t)
                if step is None or step == 0:
                    P.op("act", lambda e: e.activation(out=t1b[:], in_=st2[:, 0:TT], func=AF.Ln, scale=1.0 / D, bias=self.epsc[:, 0:1]),
                         reads=[fp + "st2", "epsc"], writes=[fp + "t1a"])
                    P.op("act", lambda e: e.activation(out=rsb[:], in_=t1b[:], func=AF.Exp, scale=-0.5), reads=[fp + "t1a"], writes=[fp + "rsb"])
                dcs = range(8) if step is None else (range(0) if step == 0 else range(step - 1, step))
                for dc in dcs:
                    b = dc % 2
                    P.op("dve", lambda e, dc=dc, b=b, cd=cd: e.scalar_tensor_tensor(out=tmpf[b][:], in0=oT[:, dc, :], scalar=self.mod(2, j, dc, cd),
                                                                                     in1=rsb[:], op0=ALU.mult, op1=ALU.mult),
                         reads=[fp + f"oT{dc}", fp + "rsb", "mods"], writes=[fp + f"tmp{b}"])
                    P.op("dve", lambda e, dc=dc, b=b, s3=s3: e.tensor_tensor(out=xT[s3][:, dc, :], in0=xT[s3][:, dc, :], in1=tmpf[b][:], op=ALU.add),
                         reads=[xr(t), fp + f"tmp{b}"], writes=[xr(t)])

            def store(t):
                s3 = t % 3
                if which == 0:
                    P.dma("sp", f"xout{s3}", lambda e, s3=s3, t=t: e.dma_start(out=self.x1s[t], in_=xT[s3][:].rearrange("p a b -> p (a b)")),
                          reads=[xr(t)], writes=[f"x1s{t}"])
                    if self.debug:
                        P.dma("sp", f"dbgx{s3}", lambda e, s3=s3, t=t: e.dma_start(out=self.dbg_x1[t], in_=xT[s3][:].rearrange("p a b -> p (a b)")),
                              reads=[xr(t)])
                else:
                    for sub in range(2):
                        r0 = t * TT + sub * 128
                        for half in range(2):
                            pb = half
                            trs = [(ptr[pb][:, jj * 128:(jj + 1) * 128], xT[s3][:, half * 4 + jj, sub * 128:(sub + 1) * 128],
                                    self.identf[:]) for jj in range(4)]
                            self.pe_transposes(trs, reads=[xr(t), "identf"], writes=[fp + f"ptr{pb}"])
                            P.op("act", lambda e, half=half, sub=sub, pb=pb: e.copy(
                                out=tok[sub][:, half * 512:(half + 1) * 512], in_=ptr[pb][:]),
                                reads=[fp + f"ptr{pb}"], writes=[fp + f"tok{sub}"])
                        P.dma("sp", f"tok{sub}", lambda e, sub=sub, r0=r0: e.dma_start(out=self.y_all[r0:r0 + 128, :], in_=tok[sub][:]),
                              reads=[fp + f"tok{sub}"])

            load(0)
            transp_in(0)
            if which == 0:
                load(1)
            else:
                load(1)
            squares(0)
            stat_mm(0)
            norm_apply(0)
            for t in range(ntl):
                def h1(fc, t=t):
                    if t >= 1 and fc < 8:
                        post(t - 1, fc + 1)
                mm1(t, 0, 4, hook=h1)
                transp_in(t + 1)
                if which == 0:
                    load(t + 2)
                squares(t + 1)
                mm1(t, 4, NFC, hook=h1)
                if t >= 1:
                    store(t - 1)
                if which == 1:
                    load(t + 2)
                stat_mm(t + 1)
                mm2(t, hook=lambda dc, t=t: norm_apply(t + 1, dc) if dc <= 4 else None)
                post(t, 0)
            for k in range(8):
                post(ntl - 1, k + 1)
            store(ntl - 1)
            P.barrier()
            P.emit()

    def phase_mixer(self, grp):
        P = self.P
        nc = self.nc
        sample = grp == 0
        tiles = list(range(0, NTS)) if sample else list(range(NTS, NT))
        ntl = len(tiles)
        ntok = ntl * TT
        tok_base = tiles[0] * TT
        nseq = 1 if sample else 4
        L = LS if sample else LP
        nkt_new = ntok // 128
        nkt = nkt_new + (PAST // 128 if sample else 0)
        nkeys = nkt * 128
        cd = 0 if sample else 1
        g = f"g{grp}"
        with ExitStack() as mx:
            attnT = self.T(mx, g + "attnT", [128, 4, ntok], BF16)
            with ExitStack() as ph:
                b1 = ExitStack()
                QT = self.T(ph, g + "QT", [128, 4, ntok], BF16)
                KT = self.T(ph, g + "KT", [128, 2, nkeys], BF16)
                VA = self.T(ph, g + "VA", [128, nkt, 2, 192], BF16)
                Win = self.T(b1, g + "Win", [128, 8, 1792], BF16)
                gv = self.T(b1, g + "gv", [128, 640], F32)
                xT = [self.T(b1, g + f"xT{i}", [128, 8, TT], F32) for i in range(2)]
                hT = [self.T(b1, g + f"hT{i}", [128, 8, TT], BF16) for i in range(2)]
                qk = [self.T(b1, g + f"qk{i}", [128, 640], F32) for i in range(2)]
                sqq = self.T(b1, g + "sqq", [128, 640], F32)
                qn = qk
                ssq = self.T(b1, g + "ssq", [128, 20], F32)
                ssq2 = self.T(b1, g + "ssq2", [128, 20], F32)
                rs = self.T(b1, g + "rs", [128, 20], F32)
                sq8 = self.T(b1, g + "sq8", [128, 8, TT], BF16)
                cs = [self.T(b1, g + f"cs{i}", [128, 128], F32) for i in range(2)] if sample else None
                r1 = sqq
                r2 = self.T(b1, g + "r2", [128, 640], F32) if sample else None
                qr = [self.T(b1, g + f"qr{i}", [128, 640], BF16) for i in range(4)]
                kdup = [self.T(b1, g + f"kdup{i}", [128, 256], BF16) for i in range(4)]
                vf = [self.T(b1, g + f"vf{i}", [128, 128], F32) for i in range(2)] if not sample else None
                xlt = [self.T(b1, g + f"xlt{i}", [128, 4, TT], F32) for i in range(1)]
                gh = self.T(b1, g + "gh", [128, 4 * TT], F32)
                gx2 = self.T(b1, g + "gx2", [128, 4 * TT], F32)
                gth = gx2
                ggt = [gx2[:].rearrange("p (a b) -> p a b", a=4)]
                ck = gh[:, 0:512].rearrange("p (a f) -> p a f", a=4)
                cv = gh[:, 512:1024].rearrange("p (a f) -> p a f", a=4)
                bufs = dict(
                    t1=self.T(b1, g + "t1", [128, TT], F32), rstd=self.T(b1, g + "rstd", [128, TT], F32),
                    tmp=[self.T(b1, g + f"tmp{i}", [128, TT], F32) for i in range(2)])
                with ExitStack() as pp:
                    bufs["st"] = self.PS(pp, g + "st", [128, 512], F32)
                    pq = [self.PS(pp, g + f"pq{i}", [128, 512], F32) for i in range(2)]
                    pkv = [self.PS(pp, g + f"pkv{i}", [128, 512], F32) for i in range(2)]
                    pxg = [self.PS(pp, g + f"pxg{i}", [128, 512], F32) for i in range(2)]
                    ptqk = self.PS(pp, g + "ptqk", [128, 1024], BF16)
                    ptq = ptqk[:, 0:512]
                    ptk = ptqk[:, 512:768]
                    winv = self.w_in.rearrange("(k p) f -> p k f", p=128)
                    for wi, (c0, c1) in enumerate(((0, 768), (768, 1280), (1280, 1792))):
                        P.dma("pool", f"win{wi}", lambda e, c0=c0, c1=c1: e.dma_start(out=Win[:, :, c0:c1], in_=winv[:, :, c0:c1]),
                              writes=[g + f"Win{wi}"])
                    winres = [g + "Win0"]
                    winres_x = [g + "Win1", g + "Win2"]
                    if getattr(self, "after_win_hook", None) is not None:
                        self.after_win_hook()
                    gvb = AP(self.gvec.tensor, 0, [[0, 128], [1, 640]])
                    P.dma("sp", "gv", lambda e: e.dma_start(out=gv[:], in_=gvb), writes=[g + "gv"])
                    P.op("pool", lambda e: e.memset(VA[:, :, :, 64:128], 1.0), writes=[g + "VA"])
                    if sample:
                        P.dma("sp", "ck", lambda e: e.dma_start(out=ck[:], in_=self.cache_k.rearrange("(a p) f -> p a f", p=128)),
                              writes=[g + "gh"])
                        P.dma("sp", "cv", lambda e: e.dma_start(out=cv[:], in_=self.cache_v.rearrange("(a p) f -> p a f", p=128)),
                              writes=[g + "gh"])
                        for a in range(4):
                            kt = nkt_new + a
                            b = a % 2
                            P.op("dve", lambda e, a=a, b=b: e.tensor_copy(
                                out=kdup[b][:].rearrange("p (k d f) -> p k d f", k=2, d=2),
                                in_=cust(ck[:, a, :], [[64, 2], [0, 2], [1, 64]])),
                                reads=[g + "gh"], writes=[g + f"kdup{b}"])
                            self.pe_transposes([(ptk[:, 0:128], kdup[b][:, 0:128], self.identb[:]),
                                                (ptk[:, 128:256], kdup[b][:, 128:256], self.identb[:])],
                                               reads=[g + f"kdup{b}", "identb"], writes=[g + "ptqk"])
                            P.op("dve", lambda e, kt=kt: e.tensor_copy(out=KT[:, :, kt * 128:(kt + 1) * 128],
                                                                       in_=ptk[:].rearrange("p (k t) -> p k t", k=2)),
                                 reads=[g + "ptqk"], writes=[g + "KT"])
                            for off in (0, 128):
                                P.op("dve", lambda e, a=a, kt=kt, off=off: e.tensor_copy(
                                    out=VA[:, kt, :, off:off + 64], in_=cv[:, a, :].rearrange("p (k f) -> p k f", k=2)),
                                    reads=[g + "gh"], writes=[g + "VA"])
                    lvl = getattr(self, "lvl", 99)
                    ntl_ = min(ntl, getattr(self, "ntl", 99))
                    st = bufs["st"]
                    t1, rstd, tmp = bufs["t1"], bufs["rstd"], bufs["tmp"]

                    def b1_load(ti):
                        if ti >= ntl_:
                            return
                        t, s2 = tiles[ti], ti % 2
                        P.dma("sp", f"xin{s2}", lambda e, s2=s2, t=t: e.dma_start(out=xT[s2][:].rearrange("p a b -> p (a b)"), in_=self.x1s[t]),
                              reads=[f"x1s{t}"], writes=[g + f"mxT{s2}"])

                    def b1_squares(ti):
                        if ti >= ntl_:
                            return
                        s2 = ti % 2
                        for kc in range(8):
                            P.op("act", lambda e, kc=kc, s2=s2: e.activation(out=sq8[:, kc, :], in_=xT[s2][:, kc, :], func=AF.Square),
                                 reads=[g + f"mxT{s2}"], writes=[g + f"sq8_{kc}"])

                    def b1_stat(ti):
                        if ti >= ntl_:
                            return
                        for kc in range(8):
                            self.pe_group([(st[:, 0:TT], self.onesb[:], sq8[:, kc, :], kc == 0, kc == 7)],
                                          reads=[g + f"sq8_{kc}", "onesb"], writes=[g + "st"])

                    def b1_apply(ti):
                        if ti >= ntl_:
                            return
                        s2 = ti % 2
                        P.op("act", lambda e: e.activation(out=t1[:], in_=st[:, 0:TT], func=AF.Ln, scale=1.0 / D, bias=self.epsc[:, 0:1]),
                             reads=[g + "st", "epsc"], writes=[g + "t1"])
                        P.op("act", lambda e: e.activation(out=rstd[:], in_=t1[:], func=AF.Exp, scale=-0.5), reads=[g + "t1"], writes=[g + "rstd"])
                        for kc in range(8):
                            b = kc % 2
                            P.op("dve", lambda e, kc=kc, b=b, s2=s2: e.tensor_tensor(out=tmp[b][:], in0=xT[s2][:, kc, :], in1=rstd[:], op=ALU.mult),
                                 reads=[g + f"mxT{s2}", g + "rstd"], writes=[g + f"tmp{b}"])
                            P.op("act", lambda e, kc=kc, b=b, s2=s2: e.activation(out=hT[s2][:, kc, :], in_=tmp[b][:], func=AF.Identity,
                                                                                  scale=self.mod(0, 1, kc, cd), bias=self.mod(1, 1, kc, cd)),
                                 reads=[g + f"tmp{b}", "mods"], writes=[g + f"mhT{s2}"])


                    def b1_transposes(ti):
                        for sb in range(2):
                            lt0 = ti * TT + sb * 128
                            kt = lt0 // 128
                            qb = (ti % 2) * 2 + sb
                            trs = [(ptq[:, jj * 128:(jj + 1) * 128], qr[qb][:, jj * 128:(jj + 1) * 128], self.identb[:]) for jj in range(4)]
                            trs += [(ptk[:, 0:128], kdup[qb][:, 0:128], self.identb[:]), (ptk[:, 128:256], kdup[qb][:, 128:256], self.identb[:])]
                            self.pe_transposes(trs, reads=[g + f"qr{qb}", g + f"kdup{qb}", "identb"], writes=[g + "ptqk"])
                            P.op("dve", lambda e, lt0=lt0: e.tensor_copy(out=QT[:, :, lt0:lt0 + 128],
                                                                         in_=ptq[:].rearrange("p (a t) -> p a t", a=4)),
                                 reads=[g + "ptqk"], writes=[g + "QT"])
                            P.op("dve", lambda e, kt=kt: e.tensor_copy(out=KT[:, :, kt * 128:(kt + 1) * 128],
                                                                       in_=ptk[:].rearrange("p (k t) -> p k t", k=2)),
                                 reads=[g + "ptqk"], writes=[g + "KT"])

                    b1_load(0)
                    b1_load(1)
                    b1_squares(0)
                    b1_stat(0)
                    b1_apply(0)
                    for ti in range(ntl_):
                        t = tiles[ti]
                        p = ti % 2
                        hres = g + f"mhT{p}"
                        b1_squares(ti + 1)
                        b1_load(ti + 2)
                        for sb in range(2):
                            lt0 = ti * TT + sb * 128
                            kt = lt0 // 128
                            hsl = lambda kc: hT[p][:, kc, sb * 128:(sb + 1) * 128]
                            self.pe_group([(pq[sb][:], hsl(kc), Win[:, kc, 0:512], kc == 0, kc == 7) for kc in range(8)],
                                          reads=winres + [hres], writes=[g + f"pq{sb}"])
                            self.pe_group([(pkv[sb][:, 0:256], hsl(kc), Win[:, kc, 512:768], kc == 0, kc == 7) for kc in range(8)],
                                          reads=winres + [hres], writes=[g + f"pkv{sb}"])
                            P.op("act", lambda e, sb=sb: e.copy(out=qk[sb][:, 0:512], in_=pq[sb][:]),
                                 reads=[g + f"pq{sb}"], writes=[g + f"qk{sb}"])
                            P.op("act", lambda e, sb=sb: e.copy(out=qk[sb][:, 512:640], in_=pkv[sb][:, 0:128]),
                                 reads=[g + f"pkv{sb}"], writes=[g + f"qk{sb}"])
                            for off in (0, 128):
                                P.op("act", lambda e, sb=sb, kt=kt, off=off: e.copy(
                                    out=VA[:, kt, :, off:off + 64], in_=pkv[sb][:, 128:256].rearrange("p (k f) -> p k f", k=2)),
                                    reads=[g + f"pkv{sb}"], writes=[g + "VA"])
                            if not sample:
                                P.op("act", lambda e, sb=sb: e.copy(out=vf[sb][:], in_=pkv[sb][:, 128:256]),
                                     reads=[g + f"pkv{sb}"], writes=[g + f"vf{sb}"])
                                P.dma("sp", f"vf{sb}", lambda e, sb=sb, lt0=lt0: e.dma_start(out=self.ncv[lt0:lt0 + 128, :], in_=vf[sb][:]),
                                      reads=[g + f"vf{sb}"])
                        b1_stat(ti + 1)
                        b1_apply(ti + 1)
                        for ch in range(8):
                            hb = ch % 2
                            self.pe_group([(pxg[hb][:, 0:TT], Win[:, kc, 768 + ch * 128:768 + (ch + 1) * 128], hT[p][:, kc, :],
                                            kc == 0, kc == 7) for kc in range(8)], reads=winres_x + [hres], writes=[g + f"pxg{hb}"])
                            if ch < 4:
                                P.op("act", lambda e, ch=ch, hb=hb: e.copy(out=xlt[0][:, ch, :], in_=pxg[hb][:, 0:TT]),
                                     reads=[g + f"pxg{hb}"], writes=[g + "xlt0"])
                            else:
                                P.op("act", lambda e, ch=ch, hb=hb: e.copy(out=gh[:, (ch - 4) * TT:(ch - 3) * TT], in_=pxg[hb][:, 0:TT]),
                                     reads=[g + f"pxg{hb}"], writes=[g + "gh"])
                        if ti >= 1:
                            b1_transposes(ti - 1)
                        for sb in range(2):
                            P.op("dve", lambda e, sb=sb: e.tensor_tensor(out=sqq[:], in0=qk[sb][:], in1=qk[sb][:], op=ALU.mult),
                                 reads=[g + f"qk{sb}"], writes=[g + "sqq"])
                            P.op("dve", lambda e, sb=sb: e.tensor_reduce(out=ssq[:, sb * 10:sb * 10 + 10], in_=sqq[:].rearrange("p (h f) -> p h f", f=64),
                                                                         axis=AX.X, op=ALU.add),
                                 reads=[g + "sqq"], writes=[g + f"ssq{sb}"])
                        P.op("act", lambda e: e.activation(out=ssq2[:], in_=ssq[:], func=AF.Ln, scale=1.0 / 64, bias=self.epsc[:, 0:1]),
                             reads=[g + "ssq0", g + "ssq1", "epsc"], writes=[g + "ssq2"])
                        P.op("act", lambda e: e.activation(out=rs[:], in_=ssq2[:], func=AF.Exp, scale=-0.5),
                             reads=[g + "ssq2"], writes=[g + "rs"])
                        P.op("pool", lambda e: e.tensor_tensor(out=gx2[:], in0=gh[:], in1=gh[:], op=ALU.mult), reads=[g + "gh"], writes=[g + "gx2"])
                        P.op("pool", lambda e: e.tensor_scalar(out=gx2[:], in0=gx2[:], scalar1=0.044715, scalar2=1.0, op0=ALU.mult, op1=ALU.add),
                             reads=[g + "gx2"], writes=[g + "gx2"])
                        P.op("pool", lambda e: e.tensor_tensor(out=gx2[:], in0=gx2[:], in1=gh[:], op=ALU.mult),
                             reads=[g + "gx2", g + "gh"], writes=[g + "gx2"])
                        P.op("act", lambda e: e.activation(out=gth[:], in_=gx2[:], func=AF.Exp, scale=-2.0 * 0.7978845608028654),
                             reads=[g + "gx2"], writes=[g + "gx2"])
                        P.op("act", lambda e: e.activation(out=gth[:], in_=gth[:], func=AF.Ln, scale=1.0, bias=1.0),
                             reads=[g + "gx2"], writes=[g + "gx2"])
                        P.op("act", lambda e: e.activation(out=gth[:], in_=gth[:], func=AF.Exp, scale=-1.0),
                             reads=[g + "gx2"], writes=[g + "gx2"])
                        for sb in range(2):
                            lt0 = ti * TT + sb * 128
                            qb = (ti % 2) * 2 + sb
                            P.op("dve", lambda e, sb=sb: e.tensor_tensor(
                                out=qn[sb][:].rearrange("p (h f) -> p h f", f=64), in0=qk[sb][:].rearrange("p (h f) -> p h f", f=64),
                                in1=cust(rs[:, sb * 10:sb * 10 + 10], [[1, 10], [0, 64]]), op=ALU.mult),
                                reads=[g + f"qk{sb}", g + "rs"], writes=[g + f"qk{sb}"])
                            P.op("dve", lambda e, sb=sb: e.tensor_tensor(out=qn[sb][:], in0=qn[sb][:], in1=gv[:], op=ALU.mult),
                                 reads=[g + f"qk{sb}", g + "gv"], writes=[g + f"qk{sb}"])
                            if not sample:
                                P.dma("sp", f"kf{sb}", lambda e, sb=sb, lt0=lt0: e.dma_start(out=self.nck[lt0:lt0 + 128, :], in_=qn[sb][:, 512:640]),
                                      reads=[g + f"qk{sb}"])
                                P.op("dve", lambda e, sb=sb, qb=qb: e.tensor_copy(out=qr[qb][:], in_=qn[sb][:]),
                                     reads=[g + f"qk{sb}"], writes=[g + f"qr{qb}"])
                            else:
                                P.dma("sp", f"cs{sb}", lambda e, sb=sb, lt0=lt0: e.dma_start(out=cs[sb][:], in_=self.rope[lt0:lt0 + 128, :]),
                                      writes=[g + f"cs{sb}"])
                                P.op("dve", lambda e, sb=sb: e.tensor_tensor(
                                    out=r1[:].rearrange("p (h f) -> p h f", f=64), in0=qn[sb][:].rearrange("p (h f) -> p h f", f=64),
                                    in1=cust(cs[sb][:, 0:64], [[0, 10], [1, 64]]), op=ALU.mult),
                                    reads=[g + f"qk{sb}", g + f"cs{sb}"], writes=[g + "sqq"])
                                for h in range(2):
                                    P.op("dve", lambda e, sb=sb, h=h: e.tensor_tensor(
                                        out=cust(r2[:, h * 16:h * 16 + 16], [[64, 10], [32, 2], [1, 16]]),
                                        in0=cust(qn[sb][:, (1 - h) * 16:(1 - h) * 16 + 16], [[64, 10], [32, 2], [1, 16]]),
                                        in1=cust(cs[sb][:, 64 + h * 16:64 + h * 16 + 16], [[0, 10], [32, 2], [1, 16]]), op=ALU.mult),
                                        reads=[g + f"qk{sb}", g + f"cs{sb}"], writes=[g + "r2"])
                                P.op("dve", lambda e, qb=qb: e.tensor_tensor(out=qr[qb][:], in0=r1[:], in1=r2[:], op=ALU.add),
                                     reads=[g + "sqq", g + "r2"], writes=[g + f"qr{qb}"])
                            P.op("dve", lambda e, qb=qb: e.tensor_copy(
                                out=kdup[qb][:].rearrange("p (k d f) -> p k d f", k=2, d=2),
                                in_=cust(qr[qb][:, 512:640], [[64, 2], [0, 2], [1, 64]])),
                                reads=[g + f"qr{qb}"], writes=[g + f"kdup{qb}"])
                        P.op("pool", lambda e: e.tensor_tensor(out=gx2[:], in0=gth[:], in1=gh[:], op=ALU.mult),
                             reads=[g + "gx2", g + "gh"], writes=[g + "gx2"])
                        c0 = t * TT
                        P.dma("sp", f"xlo{p}", lambda e, c0=c0: e.dma_start(
                            out=self.xls.rearrange("c p t -> p c t")[:, :, c0:c0 + TT], in_=xlt[0][:]),
                            reads=[g + "xlt0"], writes=[f"xls{t}"])
                        P.dma("sp", f"ggo{p}", lambda e, c0=c0: e.dma_start(
                            out=self.ggs.rearrange("c p t -> p c t")[:, :, c0:c0 + TT], in_=ggt[0][:]),
                            reads=[g + "gx2"], writes=[f"ggs{t}"])
                    b1_transposes(ntl_ - 1)
                    P.barrier()
                    P.emit()
                b1.close()
                with ExitStack() as pp:
                    NQ = 512 if sample else 256
                    pS = [self.PS(pp, g + f"pS{i}", [128, 2, 512], F32) for i in range(2)]
                    pO = [[self.PS(pp, g + f"pO{i}{h}", [128, 512], F32) for h in range(2)] for i in range(2)]
                    pT = [self.T(pp, g + f"pT{i}", [128, 2, 512], BF16) for i in range(2)]
                    rec = [self.T(pp, g + f"rec{i}", [128, 512], F32) for i in range(2)]
                    blocks = []
                    if sample:
                        for qb in range(LS // 512):
                            blocks.append((qb * 512, list(range(nkt))))
                    else:
                        for s in range(4):
                            blocks.append((s * LP, [2 * s, 2 * s + 1]))
                    it = 0
                    if lvl < 17:
                        blocks = []
                    for (q0, kts) in blocks[:getattr(self, "nqb", 99)]:
                        for pair in range(4):
                            kv = pair // 2
                            ob = it % 2
                            it += 1

                            def qk_mm(i, kt):
                                sbk = i % 2
                                mms = [(pS[sbk][:, h, 0:NQ], KT[h * 64:(h + 1) * 64, kv, kt * 128:(kt + 1) * 128],
                                        QT[h * 64:(h + 1) * 64, pair, q0:q0 + NQ], True, True) for h in range(2)]
                                self.pe_group(mms, reads=[g + "KT", g + "QT"], writes=[g + f"pS{sbk}"])
                            qk_mm(0, kts[0])
                            for i, kt in enumerate(kts):
                                sbk = i % 2
                                if i + 1 < len(kts):
                                    qk_mm(i + 1, kts[i + 1])
                                P.op("act", lambda e, sbk=sbk: e.activation(out=pT[sbk][:, :, 0:NQ], in_=pS[sbk][:, :, 0:NQ],
                                                                            func=AF.Exp, scale=0.125),
                                     reads=[g + f"pS{sbk}"], writes=[g + f"pT{sbk}"])
                                mms = [(pO[ob][h][:, 0:NQ], VA[:, kt, kv, h * 64:h * 64 + 128], pT[sbk][:, h, 0:NQ],
                                        i == 0, i == len(kts) - 1) for h in range(2)]
                                self.pe_group(mms, reads=[g + "VA", g + f"pT{sbk}"], writes=[g + f"pO{ob}0", g + f"pO{ob}1"])
                            P.op("dve", lambda e, ob=ob: e.reciprocal(out=rec[0][0:64, 0:NQ], in_=pO[ob][0][64:128, 0:NQ]),
                                 reads=[g + f"pO{ob}0"], writes=[g + "rec0"])
                            P.op("dve", lambda e, ob=ob, pair=pair, q0=q0: e.tensor_tensor(
                                out=attnT[0:64, pair, q0:q0 + NQ], in0=pO[ob][0][0:64, 0:NQ], in1=rec[0][0:64, 0:NQ], op=ALU.mult),
                                reads=[g + f"pO{ob}0", g + "rec0"], writes=[g + "attnT"])
                            P.op("dve", lambda e, ob=ob: e.reciprocal(out=rec[1][64:128, 0:NQ], in_=pO[ob][1][0:64, 0:NQ]),
                                 reads=[g + f"pO{ob}1"], writes=[g + "rec1"])
                            P.op("dve", lambda e, ob=ob, pair=pair, q0=q0: e.tensor_tensor(
                                out=attnT[64:128, pair, q0:q0 + NQ], in0=pO[ob][1][64:128, 0:NQ], in1=rec[1][64:128, 0:NQ], op=ALU.mult),
                                reads=[g + f"pO{ob}1", g + "rec1"], writes=[g + "attnT"])
                    P.barrier()
                    P.emit()
            lruT = self.T(mx, g + "lruT", [128, 4, ntok], BF16)
            if lvl < 18:
                return
            self.lru(grp, mx, lruT, tiles, nseq, L, tok_base, ntok)
            if lvl < 19:
                return
            with ExitStack() as ph:
                Wo = self.T(ph, g + "Wo", [128, 8, D], BF16)
                xT = [self.T(ph, g + f"oxT{i}", [128, 8, TT], F32) for i in range(3)]
                oT = self.T(ph, g + "ooT", [128, 8, TT], F32)
                sq = [self.T(ph, g + f"osq{i}", [128, TT], BF16) for i in range(2)]
                t1 = self.T(ph, g + "ot1", [128, TT], F32)
                rstd = self.T(ph, g + "orstd", [128, TT], F32)
                tmpf = [self.T(ph, g + f"otmpf{i}", [128, TT], F32) for i in range(2)]
                st2 = self.PS(ph, g + "ost2", [128, 512], F32)
                po = [self.PS(ph, g + f"opo{i}", [128, 512], F32) for i in range(2)]
                wov = self.w_out.rearrange("(k p) f -> p k f", p=128)
                for kc in range(8):
                    P.dma("pool", f"win{kc}", lambda e, kc=kc: e.dma_start(out=Wo[:, kc, :], in_=wov[:, kc, :]), writes=[g + f"Wo{kc}"])
                wores = [g + f"Wo{kc}" for kc in range(8)]
                xr = lambda ti: g + f"oxT{ti % 3}"

                def o_load(ti):
                    if ti >= ntl:
                        return
                    t, s3 = tiles[ti], ti % 3
                    P.dma("sp", f"xin{s3}", lambda e, s3=s3, t=t: e.dma_start(out=xT[s3][:].rearrange("p a b -> p (a b)"), in_=self.x1s[t]),
                          reads=[f"x1s{t}"], writes=[xr(ti)])

                def o_stat(dc):
                    b = dc % 2
                    self.pe_group([(st2[:, 0:TT], self.onesb[:], sq[b][:], dc == 0, dc == 7)], reads=[g + f"osq{b}", "onesb"], writes=[g + "ost2"])

                def o_post_step(ti, dc):
                    s3, b = ti % 3, dc % 2
                    P.op("dve", lambda e, dc=dc, b=b: e.scalar_tensor_tensor(out=tmpf[b][:], in0=oT[:, dc, :], scalar=self.mod(2, 1, dc, cd),
                                                                             in1=rstd[:], op0=ALU.mult, op1=ALU.mult),
                         reads=[g + f"ooT{dc}", g + "orstd", "mods"], writes=[g + f"otmpf{b}"])
                    P.op("dve", lambda e, dc=dc, b=b, s3=s3: e.tensor_tensor(out=xT[s3][:, dc, :], in0=xT[s3][:, dc, :], in1=tmpf[b][:], op=ALU.add),
                         reads=[xr(ti), g + f"otmpf{b}"], writes=[xr(ti)])

                def o_store(ti):
                    t, s3 = tiles[ti], ti % 3
                    P.dma("sp", f"xout{s3}", lambda e, s3=s3, t=t: e.dma_start(out=self.x2s[t], in_=xT[s3][:].rearrange("p a b -> p (a b)")),
                          reads=[xr(ti)], writes=[f"x2s{t}"])
                    if self.debug:
                        P.dma("sp", f"dbgx{s3}", lambda e, s3=s3, t=t: e.dma_start(out=self.dbg_x2[t], in_=xT[s3][:].rearrange("p a b -> p (a b)")),
                              reads=[xr(ti)])

                o_load(0)
                o_load(1)
                for ti in range(ntl):
                    l0 = ti * TT
                    for dc in range(8):
                        b = dc % 2
                        mms = []
                        for kc in range(8):
                            src = attnT[:, kc, l0:l0 + TT] if kc < 4 else lruT[:, kc - 4, l0:l0 + TT]
                            mms.append((po[b][:, 0:TT], Wo[:, kc, dc * 128:(dc + 1) * 128], src, kc == 0, kc == 7))
                        self.pe_group(mms, reads=wores + [g + "attnT", g + "lruT"], writes=[g + f"opo{b}"])
                        if dc >= 1:
                            o_stat(dc - 1)
                        if ti >= 1:
                            o_post_step(ti - 1, dc)
                        P.op("dve", lambda e, dc=dc, b=b: e.tensor_copy(out=oT[:, dc, :], in_=po[b][:, 0:TT]),
                             reads=[g + f"opo{b}"], writes=[g + f"ooT{dc}"])
                        P.op("act", lambda e, dc=dc, b=b: e.activation(out=sq[b][:], in_=oT[:, dc, :], func=AF.Square),
                             reads=[g + f"ooT{dc}"], writes=[g + f"osq{b}"])
                    o_stat(7)
                    if ti >= 1:
                        o_store(ti - 1)
                    o_load(ti + 2)
                    P.op("act", lambda e: e.activation(out=t1[:], in_=st2[:, 0:TT], func=AF.Ln, scale=1.0 / D, bias=self.epsc[:, 0:1]),
                         reads=[g + "ost2", "epsc"], writes=[g + "ot1"])
                    P.op("act", lambda e: e.activation(out=rstd[:], in_=t1[:], func=AF.Exp, scale=-0.5), reads=[g + "ot1"], writes=[g + "orstd"])
                for dc in range(8):
                    o_post_step(ntl - 1, dc)
                o_store(ntl - 1)
                P.barrier()
                P.emit()

    def lru(self, grp, mx, lruT, tiles, nseq, L, tok_base, ntok):
        P = self.P
        par = self.par
        sample = grp == 0
        g = f"l{grp}"
        SEG = 512
        nseg = max(1, L // SEG)
        with ExitStack() as ph:
            bdf = self.T(ph, g + "bdf", [128, 16 * 128], F32)
            bdb = self.T(ph, g + "bdb", [128, 16, 128], BF16)
            xlps = [self.T(ph, g + f"xlp{i}", [128, nseq, L + 4], F32) for i in range(2)]
            xc = self.T(ph, g + "xc", [128, nseq, L], F32)
            xcb = self.T(ph, g + "xcb", [128, nseq, L], BF16)
            ggc = self.T(ph, g + "ggc", [128, nseq, L], F32)
            hf = self.T(ph, g + "hf", [128, nseq, L], F32)
            spx = self.T(ph, g + "spx", [128, 32], F32)
            nb = self.T(ph, g + "nb", [128, 16], F32)
            hfin = self.T(ph, g + "hfin", [128, 32], F32)
            hfo = self.T(ph, g + "hfo", [32, 128], F32)
            eri = [self.T(ph, g + f"eri{i}", [128, 2, SEG], F32) for i in range(2)]
            ri = [self.T(ph, g + f"ri{i}", [128, 2, SEG], F32) for i in range(2)]
            av = [self.T(ph, g + f"av{i}", [128, SEG], F32) for i in range(2)]
            a2 = [self.T(ph, g + f"a2{i}", [128, SEG], F32) for i in range(2)]
            bxv = [self.T(ph, g + f"bxv{i}", [128, SEG], F32) for i in range(2)]
            bv = [self.T(ph, g + f"bv{i}", [128, SEG], F32) for i in range(2)]
            hb = [self.T(ph, g + f"hb{i}", [128, SEG], F32) for i in range(2)]
            yv = [self.T(ph, g + f"yv{i}", [128, SEG], F32) for i in range(2)]
            pz = [self.PS(ph, g + f"pz{i}", [128, 2, 512], F32) for i in range(2)]
            pfin = self.PS(ph, g + "pfin", [128, 128], F32)
            P.dma("sp", "bd", lambda e: e.dma_start(out=bdf[:], in_=self.bd), writes=[g + "bdf"])
            P.op("dve", lambda e: e.tensor_copy(out=bdb[:].rearrange("p a b -> p (a b)"), in_=bdf[:]), reads=[g + "bdf"], writes=[g + "bdb"])
            P.op("act", lambda e: e.activation(out=spx[:, 0:8], in_=par[:, C_LAM:C_LAM + 8], func=AF.Exp, scale=-1.0),
                 reads=["par"], writes=[g + "spx"])
            P.op("act", lambda e: e.activation(out=spx[:, 8:16], in_=spx[:, 0:8], func=AF.Ln, bias=1.0, scale=1.0),
                 reads=[g + "spx"], writes=[g + "spx"])
            P.op("dve", lambda e: e.tensor_scalar(out=spx[:, 16:24], in0=spx[:, 8:16], scalar1=-8.0, scalar2=None, op0=ALU.mult),
                 reads=[g + "spx"], writes=[g + "spx"])
            P.op("dve", lambda e: e.tensor_scalar(out=spx[:, 24:32], in0=spx[:, 8:16], scalar1=-16.0, scalar2=None, op0=ALU.mult),
                 reads=[g + "spx"], writes=[g + "spx"])
            P.op("dve", lambda e: e.tensor_scalar(out=nb[:], in0=par[:, C_BA:C_BA + 16], scalar1=-1.0, scalar2=None, op0=ALU.mult),
                 reads=["par"], writes=[g + "nb"])
            for i in range(2):
                P.op("pool", lambda e, i=i: e.memset(xlps[i][:, :, 0:2], 0.0), writes=[g + f"xlp{i}"])
                P.op("pool", lambda e, i=i: e.memset(xlps[i][:, :, L + 2:L + 4], 0.0), writes=[g + f"xlp{i}"])
            for c in range(4):
                xlp = xlps[c % 2]
                xres = g + f"xlp{c % 2}"

                def xl_load(cc):
                    if cc >= 4:
                        return
                    P.dma("sp", f"xlp{cc % 2}", lambda e, cc=cc: e.dma_start(
                        out=xlps[cc % 2][:, :, 2:2 + L], in_=self.xls[cc][:, tok_base:tok_base + ntok].rearrange("p (s t) -> p s t", s=nseq)),
                        reads=[f"xls{t}" for t in tiles], writes=[g + f"xlp{cc % 2}"])
                if c == 0:
                    xl_load(0)
                xl_load(c + 1)
                P.dma("sp", "ggc", lambda e, c=c: e.dma_start(
                    out=ggc[:], in_=self.ggs[c][:, tok_base:tok_base + ntok].rearrange("p (s t) -> p s t", s=nseq)),
                    reads=[f"ggs{t}" for t in tiles], writes=[g + "ggc"])
                cwv = [par[:, C_CONVW + c * 4 + jj:C_CONVW + c * 4 + jj + 1] for jj in range(4)]
                cbv = par[:, C_CONVB + c:C_CONVB + c + 1]
                PW = 1024 if sample else 512
                npieces = (nseq * L) // PW

                def conv_piece(k, cwv=cwv, cbv=cbv, xlp=xlp, xres=xres):
                    if sample:
                        src = lambda jj: xlp[:, 0, k * PW + jj:k * PW + jj + PW]
                        dst, dstb = xc[:, 0, k * PW:(k + 1) * PW], xcb[:, 0, k * PW:(k + 1) * PW]
                    else:
                        src = lambda jj: xlp[:, 2 * k:2 * k + 2, jj:jj + LP]
                        dst, dstb = xc[:, 2 * k:2 * k + 2, :], xcb[:, 2 * k:2 * k + 2, :]
                    P.op("act", lambda e: e.activation(out=dst, in_=src(0), func=AF.Identity, scale=cwv[0], bias=cbv),
                         reads=[xres, "par"], writes=[g + f"xcp{k}"])
                    for jj in range(1, 4):
                        P.op("dve", lambda e, jj=jj: e.scalar_tensor_tensor(out=dst, in0=src(jj), scalar=cwv[jj], in1=dst, op0=ALU.mult, op1=ALU.add),
                             reads=[xres, g + f"xcp{k}", "par"], writes=[g + f"xcp{k}"])
                    P.op("pool", lambda e: e.tensor_copy(out=dstb, in_=dst), reads=[g + f"xcp{k}"], writes=[g + f"xcbp{k}"])
                xcf = xc[:].rearrange("p s t -> p (s t)")
                xcbf = xcb[:].rearrange("p s t -> p (s t)")
                hff = hf[:].rearrange("p s t -> p (s t)")
                ggcf = ggc[:].rearrange("p s t -> p (s t)")
                units = []
                if sample:
                    for d in range(2):
                        segs = list(range(nseg)) if d == 0 else list(range(nseg - 1, -1, -1))
                        for si, sg_ in enumerate(segs):
                            units.append((d, sg_ * SEG, [(0, 0, SEG)], si))
                else:
                    for sp in range(2):
                        for d in range(2):
                            units.append((d, sp * SEG, [(2 * sp, 0, LP), (2 * sp + 1, LP, LP)], 0))
                def stage(it, which_stage, c=c, units=units):
                    d, f0, parts, si = units[it]
                    b = it % 2
                    dcol = d * 4 + c
                    if which_stage == 2:
                        return stage2(it, d, f0, parts, si, b, dcol, c)
                    ia = (d * 2 + 0) * 4 + c
                    ix = (d * 2 + 1) * 4 + c
                    self.pe_group([(pz[b][:, 0, 0:SEG], bdb[:, ia, :], xcbf[:, f0:f0 + SEG], True, True),
                                   (pz[b][:, 1, 0:SEG], bdb[:, ix, :], xcbf[:, f0:f0 + SEG], True, True)],
                                  reads=[g + "bdb", g + f"xcbp{f0 // PW}"], writes=[g + f"pz{b}"])
                    P.op("act", lambda e, b=b, dcol=dcol: e.activation(out=eri[b][:, 0, :], in_=pz[b][:, 0, 0:SEG], func=AF.Exp,
                                                                     scale=-1.0, bias=nb[:, dcol:dcol + 1]),
                         reads=[g + f"pz{b}", g + "nb"], writes=[g + f"eri{b}"])
                    P.op("act", lambda e, b=b, dcol=dcol: e.activation(out=eri[b][:, 1, :], in_=pz[b][:, 1, 0:SEG], func=AF.Exp,
                                                                     scale=-1.0, bias=nb[:, 8 + dcol:8 + dcol + 1]),
                         reads=[g + f"pz{b}", g + "nb"], writes=[g + f"eri{b}"])
                    P.op("act", lambda e, b=b: e.activation(out=eri[b][:], in_=eri[b][:], func=AF.Ln, scale=1.0, bias=1.0),
                         reads=[g + f"eri{b}"], writes=[g + f"eri{b}"])
                    P.op("act", lambda e, b=b: e.activation(out=ri[b][:], in_=eri[b][:], func=AF.Exp, scale=-1.0),
                         reads=[g + f"eri{b}"], writes=[g + f"ri{b}"])
                    P.op("act", lambda e, b=b, dcol=dcol: e.activation(out=av[b][:], in_=ri[b][:, 0, :], func=AF.Exp,
                                                                     scale=spx[:, 16 + dcol:16 + dcol + 1]),
                         reads=[g + f"ri{b}", g + "spx"], writes=[g + f"av{b}"])
                    P.op("dve", lambda e, b=b: e.tensor_tensor(out=a2[b][:], in0=av[b][:], in1=av[b][:], op=ALU.mult),
                         reads=[g + f"av{b}"], writes=[g + f"a2{b}"])
                    P.op("dve", lambda e, b=b, f0=f0: e.tensor_tensor(out=bxv[b][:], in0=ri[b][:, 1, :], in1=xcf[:, f0:f0 + SEG], op=ALU.mult),
                         reads=[g + f"ri{b}", g + f"xcp{f0 // PW}"], writes=[g + f"bxv{b}"])

                def stage2(it, d, f0, parts, si, b, dcol, c):
                    P.op("act", lambda e, b=b: e.activation(out=a2[b][:], in_=a2[b][:], func=AF.Ln, scale=-1.0, bias=1.0),
                         reads=[g + f"a2{b}"], writes=[g + f"a2{b}"])
                    P.op("act", lambda e, b=b: e.activation(out=a2[b][:], in_=a2[b][:], func=AF.Exp, scale=0.5),
                         reads=[g + f"a2{b}"], writes=[g + f"a2{b}"])
                    P.op("dve", lambda e, b=b: e.tensor_tensor(out=bv[b][:], in0=bxv[b][:], in1=a2[b][:], op=ALU.mult),
                         reads=[g + f"bxv{b}", g + f"a2{b}"], writes=[g + f"bv{b}"])
                    for (s, off, plen) in parts:
                        if sample and si == 0:
                            init, irs = par[:, C_H0 + dcol:C_H0 + dcol + 1], ["par"]
                        elif sample:
                            init, irs = (hff[:, f0 - 1:f0], [g + "hf"]) if d == 0 else (hb[1 - b][:, 0:1], [g + f"hb{1 - b}"])
                        else:
                            init, irs = 0.0, []
                        if d == 0:
                            P.op("dve", lambda e, b=b, f0=f0, off=off, plen=plen, init=init: e.tensor_tensor_scan(
                                out=hff[:, f0 + off:f0 + off + plen], data0=av[b][:, off:off + plen], data1=bv[b][:, off:off + plen],
                                initial=init, op0=ALU.mult, op1=ALU.add),
                                reads=[g + f"av{b}", g + f"bv{b}", g + "hf"] + irs, writes=[g + "hf"])
                            if not sample:
                                P.op("pool", lambda e, s=s, dcol=dcol, f0=f0, off=off, plen=plen: e.tensor_copy(
                                    out=hfin[:, s * 8 + dcol:s * 8 + dcol + 1], in_=hff[:, f0 + off + plen - 1:f0 + off + plen]),
                                    reads=[g + "hf"], writes=[g + "hfin"])
                        else:
                            rev = lambda tl, off=off, plen=plen: cust(tl[:, off + plen - 1:off + plen], [[-1, plen]])
                            P.op("dve", lambda e, b=b, init=init, rev=rev: e.tensor_tensor_scan(
                                out=rev(hb[b]), data0=rev(av[b]), data1=rev(bv[b]), initial=init, op0=ALU.mult, op1=ALU.add),
                                reads=[g + f"av{b}", g + f"bv{b}"] + irs, writes=[g + f"hb{b}"])
                            if not sample:
                                P.op("pool", lambda e, s=s, dcol=dcol, b=b, off=off: e.tensor_copy(
                                    out=hfin[:, s * 8 + dcol:s * 8 + dcol + 1], in_=hb[b][:, off:off + 1]),
                                    reads=[g + f"hb{b}"], writes=[g + "hfin"])
                    if d == 1:
                        P.op("pool", lambda e, b=b, f0=f0: e.tensor_tensor(out=yv[b][:], in0=hb[b][:], in1=hff[:, f0:f0 + SEG], op=ALU.add),
                             reads=[g + f"hb{b}", g + "hf"], writes=[g + f"yv{b}"])
                        P.op("dve", lambda e, b=b, f0=f0, c=c: e.tensor_tensor(
                            out=lruT[:, c, f0:f0 + SEG], in0=yv[b][:], in1=ggcf[:, f0:f0 + SEG], op=ALU.mult),
                            reads=[g + f"yv{b}", g + "ggc"], writes=[f"g{grp}lruT"])

                nextp = [1]

                def pieces_upto(j):
                    need = units[j][1] // PW + 1
                    while nextp[0] < npieces and nextp[0] <= need:
                        conv_piece(nextp[0])
                        nextp[0] += 1
                conv_piece(0)
                stage(0, 1)
                pieces_upto(0)
                for it in range(len(units)):
                    if it + 1 < len(units):
                        stage(it + 1, 1)
                        pieces_upto(it + 1)
                    stage(it, 2)
            if not sample:
                self.pe_transposes([(pfin[0:32, :], hfin[:, 0:32], self.identf[:])], reads=[g + "hfin", "identf"], writes=[g + "pfin"])
                P.op("act", lambda e: e.copy(out=hfo[:], in_=pfin[0:32, :]), reads=[g + "pfin"], writes=[g + "hfo"])
                P.dma("sp", "hfo", lambda e: e.dma_start(out=self.nst, in_=hfo[:]), reads=[g + "hfo"])
            P.barrier()
            P.emit()


_CACHE = {}


def _rope_table():
    pos = np.arange(LS)
    r = (pos // 64).astype(np.float32)
    c = (pos % 64).astype(np.float32)
    n_freq = 16
    inv = (np.float32(10000.0) ** (-np.arange(n_freq, dtype=np.float32) / np.float32(n_freq))).astype(np.float32)
    ar = r[:, None] * inv
    ac = c[:, None] * inv
    ang = np.concatenate([ar, ar, ac, ac], axis=-1).astype(np.float32)
    cos = np.cos(ang).astype(np.float32)
    sin = np.sin(ang).astype(np.float32)
    sgn = np.concatenate([-np.ones(16), np.ones(16), -np.ones(16), np.ones(16)]).astype(np.float32)
    return np.ascontiguousarray(np.concatenate([cos, sin * sgn[None, :]], axis=-1).astype(np.float32))


def _pack_inputs(inp):
    f = lambda a: np.ascontiguousarray(np.asarray(a, dtype=np.float32))
    x_prompt, x_sample = f(inp["x_prompt"]), f(inp["x_sample"])
    c, c_ctx = f(inp["c"]), f(inp["c_ctx"])
    fm = lambda v: v.reshape(-1, 128).T
    shared = {
        "w_mod": f(inp["w_mod"])[0], "ffn_w_in": f(inp["ffn_w_in"])[0], "ffn_w_out": f(inp["ffn_w_out"])[0],
        "w_in": f(inp["w_in"])[0], "w_out": f(inp["w_out"])[0],
        "gvec": np.ascontiguousarray(np.concatenate([np.tile(f(inp["q_norm"])[0], 8), np.tile(f(inp["k_norm"])[0], 2)])),
        "rope": _rope_table(),
    }
    wa, wx = f(inp["lru_wa"])[0], f(inp["lru_wx"])[0]
    bd = np.zeros((128, 16, 128), np.float32)
    for d in range(2):
        for gi, w in enumerate((wa, wx)):
            for ch in range(4):
                i = (d * 2 + gi) * 4 + ch
                bd[0:64, i, 0:64] = w[d, 2 * ch]
                bd[64:128, i, 64:128] = w[d, 2 * ch + 1]
    shared["bd"] = np.ascontiguousarray(bd.reshape(128, 16 * 128))
    pbase = np.zeros((128, NPAR), np.float32)
    pbase[:, C_BMOD:C_BMOD + 72] = fm(f(inp["b_mod"])[0])
    pbase[:, C_NPRE:C_NPRE + 24] = fm(f(inp["norm_pre"])[0].reshape(-1))
    pbase[:, C_NPOST:C_NPOST + 24] = fm(f(inp["norm_post"])[0].reshape(-1))
    cw = f(inp["conv_w"])[0]
    pbase[:, C_CONVW:C_CONVW + 16] = cw.reshape(4, 4, 128).transpose(2, 1, 0).reshape(128, 16)
    pbase[:, C_CONVB:C_CONVB + 4] = fm(f(inp["conv_b"])[0])
    pbase[:, C_BA:C_BA + 8] = fm(f(inp["lru_ba"])[0].reshape(-1))
    pbase[:, C_BX:C_BX + 8] = fm(f(inp["lru_bx"])[0].reshape(-1))
    pbase[:, C_LAM:C_LAM + 8] = fm(f(inp["lru_lambda"])[0].reshape(-1))
    cache_k, cache_v, state = f(inp["cache_k"]), f(inp["cache_v"]), f(inp["state_lru"])
    maps = []
    for core in range(NCORES):
        pr = pbase.copy()
        c2 = np.stack([fm(c[core]), fm(c_ctx)], axis=-1)
        pr[:, C_C2:C_C2 + 16] = c2.reshape(128, 16)
        pr[:, C_H0:C_H0 + 8] = fm(state[core, 0].reshape(-1))
        m = dict(shared)
        m["params"] = pr
        m["x_all"] = np.ascontiguousarray(np.concatenate([x_sample[core], x_prompt[4 * core:4 * core + 4].reshape(4 * LP, D)], axis=0))
        m["cache_k"] = np.ascontiguousarray(cache_k[core, 0].reshape(PAST, 128))
        m["cache_v"] = np.ascontiguousarray(cache_v[core, 0].reshape(PAST, 128))
        maps.append(m)
    return maps


def kernel(**inputs):
    if "nc" not in _CACHE:
        _CACHE["nc"] = Builder().build()
    nc = _CACHE["nc"]
    maps = _pack_inputs(inputs)
    res = run_bass_kernel_spmd(nc, maps, core_ids=list(range(NCORES)))
    R = res.results
    y_sample = np.stack([R[i]["y_all"][:LS] for i in range(NCORES)], axis=0)
    y_prompt = np.concatenate([R[i]["y_all"][LS:].reshape(4, LP, D) for i in range(NCORES)], axis=0)
    nck = np.concatenate([R[i]["nck"].reshape(4, 1, LP, 2, 64) for i in range(NCORES)], axis=0)
    ncv = np.concatenate([R[i]["ncv"].reshape(4, 1, LP, 2, 64) for i in range(NCORES)], axis=0)
    nst = np.concatenate([R[i]["nst"].reshape(4, 1, 2, 512) for i in range(NCORES)], axis=0)
    return (y_prompt.astype(np.float32), y_sample.astype(np.float32), nck.astype(np.float32),
            ncv.astype(np.float32), nst.astype(np.float32))
```

```python
import numpy as np
from contextlib import ExitStack
import concourse.bass as bass
import concourse.mybir as mybir
from concourse.bass_utils import run_bass_kernel_spmd
from concourse.ap import AP

F32 = mybir.dt.float32
BF16 = mybir.dt.bfloat16
ALU = mybir.AluOpType
AF = mybir.ActivationFunctionType
AX = mybir.AxisListType

ENGS = ("pe", "act", "dve", "pool", "sp")

D = 1024
TT = 256
NT = 20
NTS = 16
LS = 4096
LP = 256
PAST = 512
FF = 2816
NFC = 22
EPS = 1e-6
NCORES = 8

C_C2 = 0
C_BMOD = C_C2 + 16
C_NPRE = C_BMOD + 72
C_NPOST = C_NPRE + 24
C_CONVW = C_NPOST + 24
C_CONVB = C_CONVW + 16
C_BA = C_CONVB + 4
C_BX = C_BA + 8
C_LAM = C_BX + 8
C_H0 = C_LAM + 8
NPAR = C_H0 + 8


class Prog:
    def __init__(self, nc, stack):
        self.nc = nc
        self.stack = stack
        self.q = {e: [] for e in ENGS}
        self.cnt = {e: 0 for e in ENGS}
        self.sem = {}
        for e in ("pe", "act", "dve", "pool"):
            self.sem[e] = stack.enter_context(nc.semaphore("s_" + e))
        self.seen = {e: {} for e in ENGS}
        self.last_w = {}
        self.readers = {}
        self.chan = {}
        self.nsem = 4
        self.ninstr = 0

    def _need(self, eng, tok, out):
        if tok is None:
            return
        key, val, semh, prod = tok
        if prod == "pe" and eng == "pe":
            return
        if self.seen[eng].get(key, 0) >= val:
            return
        cur = out.get(key)
        if cur is None or cur[0] < val:
            out[key] = (val, semh)

    def _deps(self, eng, reads, writes):
        out = {}
        for r in reads:
            self._need(eng, self.last_w.get(r), out)
        for w in writes:
            self._need(eng, self.last_w.get(w), out)
            rd = self.readers.get(w)
            if rd:
                for t in rd.values():
                    self._need(eng, t, out)
        for key, (val, semh) in out.items():
            self.seen[eng][key] = val
            self.q[eng].append(("wait", semh, val))

    def _commit(self, tok, reads, writes):
        for r in reads:
            d = self.readers.setdefault(r, {})
            cur = d.get(tok[0])
            if cur is None or cur[1] < tok[1]:
                d[tok[0]] = tok
        for w in writes:
            self.last_w[w] = tok
            self.readers[w] = {}

    def op(self, eng, fn, reads=(), writes=()):
        self._deps(eng, reads, writes)
        self.cnt[eng] += 1
        semh = self.sem[eng]
        self.q[eng].append(("op", fn, semh, 1))
        tok = ("e_" + eng, self.cnt[eng], semh, eng)
        self._commit(tok, reads, writes)
        return tok

    def dma(self, eng, ch, fn, reads=(), writes=()):
        if ch not in self.chan:
            s = self.stack.enter_context(self.nc.semaphore("c_" + ch))
            self.chan[ch] = [s, 0]
            self.nsem += 1
        self._deps(eng, reads, writes)
        c = self.chan[ch]
        c[1] += 16
        self.q[eng].append(("op", fn, c[0], 16))
        tok = ("c_" + ch, c[1], c[0], "dma")
        self._commit(tok, reads, writes)
        return tok

    def barrier(self):
        for eng in ENGS:
            out = {}
            for e in ("pe", "act", "dve", "pool"):
                if e != eng and self.cnt[e] > 0:
                    self._need(eng, ("e_" + e, self.cnt[e], self.sem[e], "x"), out)
            for ch, (s, v) in self.chan.items():
                if v > 0:
                    self._need(eng, ("c_" + ch, v, s, "dma"), out)
            for key, (val, semh) in out.items():
                self.seen[eng][key] = val
                self.q[eng].append(("wait", semh, val))

    def emit(self):
        nc = self.nc
        q = self.q
        self.ninstr += sum(len(v) for v in q.values())

        def run(engine, items):
            for it in items:
                if it[0] == "wait":
                    engine.wait_ge(it[1], it[2])
                else:
                    ins = it[1](engine)
                    ins.then_inc(it[2], it[3])

        with nc.Block() as block:
            @block.tensor
            def _(e):
                run(e, q["pe"])

            @block.scalar
            def _(e):
                run(e, q["act"])

            @block.vector
            def _(e):
                run(e, q["dve"])

            @block.gpsimd
            def _(e):
                run(e, q["pool"])

            @block.sync
            def _(e):
                run(e, q["sp"])
        self.q = {e: [] for e in ENGS}


def cust(base, dims):
    return AP(base.tensor, base.offset, [list(base.ap[0])] + [list(d) for d in dims])


class Builder:
    def __init__(self, debug=False, stop_after=None):
        self.debug = debug
        self.stop_after = stop_after
        self.nc = bass.Bass("TRN2", target_bir_lowering=False)
        nc = self.nc
        di = lambda n, s: nc.dram_tensor(n, s, F32, kind="ExternalInput").ap()
        do = lambda n, s: nc.dram_tensor(n, s, F32, kind="ExternalOutput").ap()
        ds = lambda n, s, dt=F32: nc.dram_tensor(n, s, dt, kind="Internal").ap()
        self.x_all = di("x_all", [NT * TT, D])
        self.params = di("params", [128, NPAR])
        self.w_mod = di("w_mod", [D, 9 * D])
        self.ffn_w_in = di("ffn_w_in", [2, D, 2 * FF])
        self.ffn_w_out = di("ffn_w_out", [2, FF, D])
        self.w_in = di("w_in", [D, 1792])
        self.w_out = di("w_out", [D, D])
        self.gvec = di("gvec", [640])
        self.rope = di("rope", [LS, 128])
        self.cache_k = di("cache_k", [PAST, 128])
        self.cache_v = di("cache_v", [PAST, 128])
        self.bd = di("bd", [128, 16 * 128])
        self.y_all = do("y_all", [NT * TT, D])
        self.nck = do("nck", [4 * LP, 128])
        self.ncv = do("ncv", [4 * LP, 128])
        self.nst = do("nst", [32, 128])
        self.x1s = ds("x1s", [NT, 128, 8 * TT])
        self.x2s = ds("x2s", [NT, 128, 8 * TT])
        self.xls = ds("xls", [4, 128, NT * TT])
        self.ggs = ds("ggs", [4, 128, NT * TT])
        if debug:
            self.dbg_mods = do("dbg_mods", [128, 144])
            self.dbg_x1 = do("dbg_x1", [NT, 128, 8 * TT])
            self.dbg_x2 = do("dbg_x2", [NT, 128, 8 * TT])
            self.dbg_attn = do("dbg_attn", [128, 4 * NT * TT])
            self.dbg_lru = do("dbg_lru", [128, 4 * NT * TT])

    def T(self, st, name, shape, dt):
        return st.enter_context(self.nc.sbuf_tensor(name, shape, dt))

    def PS(self, st, name, shape, dt):
        return st.enter_context(self.nc.psum_tensor(name, shape, dt))

    def pe_group(self, mms, reads, writes):
        def fn(e, mms=mms):
            ins = None
            for (o, l, r, s, t) in mms:
                ins = e.matmul(o, lhsT=l, rhs=r, start=s, stop=t)
            return ins
        return self.P.op("pe", fn, reads, writes)

    def pe_transposes(self, trs, reads, writes):
        def fn(e, trs=trs):
            ins = None
            for (o, i, idn) in trs:
                ins = e.transpose(o, i, idn)
            return ins
        return self.P.op("pe", fn, reads, writes)

    def mod(self, k, j, dc, cd):
        c = ((k * 3 + j) * 8 + dc) * 2 + cd
        return self.mods[:, c:c + 1]

    def build(self):
        nc = self.nc
        with ExitStack() as top:
            self.P = Prog(nc, top)
            P = self.P
            self.par = self.T(top, "par", [128, NPAR], F32)
            self.mods = self.T(top, "mods", [128, 144], F32)
            self.identf = self.T(top, "identf", [128, 128], F32)
            self.identb = self.T(top, "identb", [128, 128], BF16)
            self.onesb = self.T(top, "onesb", [128, 128], BF16)
            P.dma("sp", "par", lambda e: e.dma_start(out=self.par[:], in_=self.params), writes=["par"])
            P.op("pool", lambda e: e.memset(self.identf[:], 0.0), writes=["identf"])
            P.op("pool", lambda e: e.affine_select(out=self.identf[:], in_=self.identf[:], pattern=[[-1, 128]],
                                                   compare_op=ALU.not_equal, fill=1.0, base=0, channel_multiplier=1),
                 reads=["identf"], writes=["identf"])
            P.op("pool", lambda e: e.tensor_copy(out=self.identb[:], in_=self.identf[:]), reads=["identf"], writes=["identb"])
            P.op("pool", lambda e: e.memset(self.onesb[:], 1.0), writes=["onesb"])
            self.epsc = self.T(top, "epsc", [128, 1], F32)
            P.op("pool", lambda e: e.memset(self.epsc[:], EPS), writes=["epsc"])

            with ExitStack() as f1:
                W1a = self.T(f1, "F0_W1", [128, 8, 2 * FF], BF16)
                w1va = self.ffn_w_in[0].rearrange("(k p) f -> p k f", p=128)
                for cb in range(4):
                    for hf in range(2):
                        c0 = hf * FF + cb * 704
                        P.dma("pool", f"w1_{cb}_{hf}", lambda e, c0=c0: e.dma_start(out=W1a[:, :, c0:c0 + 704], in_=w1va[:, :, c0:c0 + 704]),
                              writes=[f"F0_W1_{cb}_{hf}"])
                self.phase_mod()
                if self.stop_after == "mod":
                    return self.finish()
                self.phase_ffn(0, W1_pre=W1a)
            if self.stop_after == "ffn1":
                return self.finish()
            self.phase_mixer(0)
            if self.stop_after == "mix0":
                return self.finish()
            with ExitStack() as pre:
                W1n = self.T(pre, "F1_W1", [128, 8, 2 * FF], BF16)
                w1v = self.ffn_w_in[1].rearrange("(k p) f -> p k f", p=128)

                def prefetch():
                    for cb in range(4):
                        for hf in range(2):
                            c0 = hf * FF + cb * 704
                            P.dma("pool", f"w1_{cb}_{hf}", lambda e, c0=c0: e.dma_start(out=W1n[:, :, c0:c0 + 704], in_=w1v[:, :, c0:c0 + 704]),
                                  writes=[f"F1_W1_{cb}_{hf}"])
                self.after_win_hook = prefetch
                self.phase_mixer(1)
                self.after_win_hook = None
                if self.stop_after == "mix1":
                    return self.finish()
                self.phase_ffn(1, W1_pre=W1n)
            return self.finish()

    def finish(self):
        self.P.barrier()
        self.P.emit()
        return self.nc

    def phase_mod(self):
        P = self.P
        par = self.par
        with ExitStack() as ph:
            wm = [self.T(ph, f"wm{i}", [128, 8, 512], F32) for i in range(2)]
            sc = self.T(ph, "sc", [128, 16], F32)
            mT = self.T(ph, "mT", [128, 144], F32)
            tmp = self.T(ph, "mtmp", [128, 16], F32)
            pm = self.PS(ph, "pm", [128, 512], F32)
            P.op("act", lambda e: e.activation(out=sc[:], in_=par[:, C_C2:C_C2 + 16], func=AF.Silu),
                 reads=["par"], writes=["sc"])
            wmv = self.w_mod.rearrange("(k p) f -> p k f", p=128)
            for hb in range(18):
                b = hb % 2
                P.dma("sp", f"wm{b}", lambda e, b=b, hb=hb: e.dma_start(out=wm[b][:], in_=wmv[:, :, hb * 512:(hb + 1) * 512]),
                      writes=[f"wm{b}"])
                for q in range(4):
                    g = hb * 4 + q
                    mms = [(pm[:, g * 2:g * 2 + 2], wm[b][:, kc, q * 128:(q + 1) * 128], sc[:, kc * 2:kc * 2 + 2],
                            kc == 0, kc == 7) for kc in range(8)]
                    self.pe_group(mms, reads=[f"wm{b}", "sc"], writes=["pm"])
            bm = cust(par[:, C_BMOD:C_BMOD + 72], [[1, 72], [0, 2]])
            P.op("dve", lambda e: e.tensor_tensor(out=mT[:].rearrange("p (a b) -> p a b", b=2),
                                                  in0=pm[:, 0:144].rearrange("p (a b) -> p a b", b=2), in1=bm, op=ALU.add),
                 reads=["pm", "par"], writes=["mT"])
            mods = self.mods
            for j in range(3):
                rw = 1.0 if j == 1 else 0.5
                P.op("dve", lambda e, j=j: e.tensor_scalar(out=tmp[:], in0=mT[:, (3 * j + 1) * 16:(3 * j + 2) * 16],
                                                           scalar1=1.0, scalar2=None, op0=ALU.add),
                     reads=["mT"], writes=["mtmp"])
                npre = cust(par[:, C_NPRE + j * 8:C_NPRE + j * 8 + 8], [[1, 8], [0, 2]])
                P.op("dve", lambda e, j=j, npre=npre: e.tensor_tensor(
                    out=mods[:, (0 * 3 + j) * 16:(0 * 3 + j) * 16 + 16].rearrange("p (a b) -> p a b", b=2),
                    in0=tmp[:].rearrange("p (a b) -> p a b", b=2), in1=npre, op=ALU.mult),
                    reads=["mtmp", "par"], writes=["mods"])
                P.op("dve", lambda e, j=j: e.tensor_copy(out=mods[:, (1 * 3 + j) * 16:(1 * 3 + j) * 16 + 16],
                                                         in_=mT[:, (3 * j) * 16:(3 * j) * 16 + 16]),
                     reads=["mT"], writes=["mods"])
                npost = cust(par[:, C_NPOST + j * 8:C_NPOST + j * 8 + 8], [[1, 8], [0, 2]])
                P.op("dve", lambda e, j=j, npost=npost, rw=rw: e.scalar_tensor_tensor(
                    out=mods[:, (2 * 3 + j) * 16:(2 * 3 + j) * 16 + 16].rearrange("p (a b) -> p a b", b=2),
                    in0=mT[:, (3 * j + 2) * 16:(3 * j + 3) * 16].rearrange("p (a b) -> p a b", b=2),
                    scalar=rw, in1=npost, op0=ALU.mult, op1=ALU.mult),
                    reads=["mT", "par"], writes=["mods"])
            if self.debug:
                P.dma("sp", "dbg", lambda e: e.dma_start(out=self.dbg_mods, in_=mods[:]), reads=["mods"])
            P.barrier()
            P.emit()

    def prenorm(self, xT, hT, j, cd, bufs, tag, xres=None):
        P = self.P
        sq, st, t1, rstd, tmp = bufs["sq"], bufs["st"], bufs["t1"], bufs["rstd"], bufs["tmp"]
        xr, hr = (xres or tag + "xT"), tag + "hT"
        for kc in range(8):
            b = kc % 2
            P.op("act", lambda e, kc=kc, b=b: e.activation(out=sq[b][:], in_=xT[:, kc, :], func=AF.Square),
                 reads=[xr], writes=[f"sq{b}"])
            self.pe_group([(st[:, 0:TT], self.onesb[:], sq[b][:], kc == 0, kc == 7)], reads=[f"sq{b}", "onesb"], writes=["st"])
        P.op("act", lambda e: e.activation(out=t1[:], in_=st[:, 0:TT], func=AF.Ln, scale=1.0 / D, bias=self.epsc[:, 0:1]),
             reads=["st", "epsc"], writes=["t1"])
        P.op("act", lambda e: e.activation(out=rstd[:], in_=t1[:], func=AF.Exp, scale=-0.5),
             reads=["t1"], writes=["rstd"])
        for kc in range(8):
            b = kc % 2
            P.op("dve", lambda e, kc=kc, b=b: e.tensor_tensor(out=tmp[b][:], in0=xT[:, kc, :], in1=rstd[:], op=ALU.mult),
                 reads=[xr, "rstd"], writes=[f"tmp{b}"])
            P.op("act", lambda e, kc=kc, b=b: e.activation(out=hT[:, kc, :], in_=tmp[b][:], func=AF.Identity,
                                                           scale=self.mod(0, j, kc, cd), bias=self.mod(1, j, kc, cd)),
                 reads=[f"tmp{b}", "mods"], writes=[hr])

    def post(self, xT, oT, j, cd, bufs, tag):
        P = self.P
        st2, t1, rstd, tmp = bufs["st2"], bufs["t1"], bufs["rstd"], bufs["tmpf"]
        xr = tag + "xT"
        P.op("act", lambda e: e.activation(out=t1[:], in_=st2[:, 0:TT], func=AF.Ln, scale=1.0 / D, bias=self.epsc[:, 0:1]),
             reads=["st2", "epsc"], writes=["t1"])
        P.op("act", lambda e: e.activation(out=rstd[:], in_=t1[:], func=AF.Exp, scale=-0.5),
             reads=["t1"], writes=["rstd"])
        for dc in range(8):
            b = dc % 2
            P.op("dve", lambda e, dc=dc, b=b: e.scalar_tensor_tensor(out=tmp[b][:], in0=oT[:, dc, :], scalar=self.mod(2, j, dc, cd),
                                                                     in1=rstd[:], op0=ALU.mult, op1=ALU.mult),
                 reads=[f"oT{dc}", "rstd", "mods"], writes=[f"tmpf{b}"])
            P.op("pool", lambda e, dc=dc, b=b: e.tensor_tensor(out=xT[:, dc, :], in0=xT[:, dc, :], in1=tmp[b][:], op=ALU.add),
                 reads=[xr, f"tmpf{b}"], writes=[xr])

    def out_groups(self, groups_fn, oT, bufs):
        P = self.P
        po, sq, st2 = bufs["po"], bufs["sq"], bufs["st2"]
        def stat(dc):
            b = dc % 2
            self.pe_group([(st2[:, 0:TT], self.onesb[:], sq[b][:], dc == 0, dc == 7)], reads=[f"sq{b}", "onesb"], writes=["st2"])
        for dc in range(8):
            b = dc % 2
            mms, reads = groups_fn(dc, po[b][:, 0:TT])
            self.pe_group(mms, reads=reads, writes=[f"po{b}"])
            if dc >= 1:
                stat(dc - 1)
            P.op("dve", lambda e, dc=dc, b=b: e.tensor_copy(out=oT[:, dc, :], in_=po[b][:, 0:TT]), reads=[f"po{b}"], writes=[f"oT{dc}"])
            P.op("act", lambda e, b=b, dc=dc: e.activation(out=sq[b][:], in_=oT[:, dc, :], func=AF.Square),
                 reads=[f"oT{dc}"], writes=[f"sq{b}"])
        stat(7)

    def phase_ffn(self, which, W1_pre=None):
        P = self.P
        j = 0 if which == 0 else 2
        fp = f"F{which}_"
        with ExitStack() as ph:
            if W1_pre is not None:
                W1g, W1u = W1_pre[:, :, 0:FF], W1_pre[:, :, FF:2 * FF]
            else:
                W1 = self.T(ph, fp + "W1", [128, 8, 2 * FF], BF16)
                W1g, W1u = W1[:, :, 0:FF], W1[:, :, FF:2 * FF]
            W2 = self.T(ph, fp + "W2", [128, NFC, D], BF16)
            xT = [self.T(ph, fp + f"xT{i}", [128, 8, TT], F32) for i in range(3)]
            hT = [self.T(ph, fp + f"hT{i}", [128, 8, TT], BF16) for i in range(2)]
            actT = self.T(ph, fp + "actT", [128, NFC, TT], BF16)
            oT = self.T(ph, fp + "oT", [128, 8, TT], F32)
            sg = [self.T(ph, fp + f"sg{i}", [128, TT], F32) for i in range(2)]
            tok = [self.T(ph, fp + f"tok{i}", [128, D], F32) for i in range(2)]
            sq8 = self.T(ph, fp + "sq8", [128, 8, TT], BF16)
            sqp = [self.T(ph, fp + f"sqp{i}", [128, TT], BF16) for i in range(2)]
            t1a = self.T(ph, fp + "t1a", [128, TT], F32)
            rsa = self.T(ph, fp + "rsa", [128, TT], F32)
            t1b = t1a
            t1c = t1a
            rsb = self.T(ph, fp + "rsb", [128, TT], F32)
            tmp = [self.T(ph, fp + f"tmp{i}", [128, TT], F32) for i in range(2)]
            tmpf = tmp
            tmpn = [self.T(ph, fp + f"tmpn{i}", [128, TT], F32) for i in range(2)]
            st = self.PS(ph, fp + "st", [128, 512], F32)
            st2 = self.PS(ph, fp + "st2", [128, 512], F32)
            po = [self.PS(ph, fp + f"po{i}", [128, 512], F32) for i in range(2)]
            pgu = [self.PS(ph, fp + f"pgu{i}", [128, 512], F32) for i in range(2)]
            ptr = [self.PS(ph, fp + f"ptr{i}", [128, 512], F32) for i in range(2)]
            w1v = self.ffn_w_in[which].rearrange("(k p) f -> p k f", p=128)
            CB = 704
            for cb in range(4):
                for hf in range(2):
                    if W1_pre is not None:
                        continue
                    dstw = W1g if hf == 0 else W1u
                    c0 = cb * CB
                    P.dma("pool", f"w1_{cb}_{hf}", lambda e, c0=c0, hf=hf, dstw=dstw: e.dma_start(
                        out=dstw[:, :, c0:c0 + CB], in_=w1v[:, :, hf * FF + c0:hf * FF + c0 + CB]),
                        writes=[fp + f"W1_{cb}_{hf}"])
            w2v = self.ffn_w_out[which].rearrange("(f p) d -> p f d", p=128)
            for q in range(2):
                P.dma("pool", f"w2_{q}", lambda e, q=q: e.dma_start(out=W2[:, q * 11:(q + 1) * 11, :], in_=w2v[:, q * 11:(q + 1) * 11, :]),
                      writes=[fp + f"W2_{q}"])
            def w1res_for(fc):
                cbs = sorted({(fc * 128) // CB, (fc * 128 + 127) // CB})
                return [fp + f"W1_{cb}_{hf}" for cb in cbs for hf in range(2)]
            w2res = [fp + "W2_0", fp + "W2_1"]
            src = self.x2s
            ntl = getattr(self, "ntl", NT)
            xr = lambda t: fp + f"xT{t % 3}"
            hr = lambda t: fp + f"hT{t % 2}"
            cdof = lambda t: 0 if t < NTS else 1

            def load(t):
                if t >= ntl:
                    return
                if which == 0:
                    for sub in range(2):
                        r0 = t * TT + sub * 128
                        P.dma("sp", f"tok{sub}", lambda e, sub=sub, r0=r0: e.dma_start(out=tok[sub][:], in_=self.x_all[r0:r0 + 128, :]),
                              writes=[fp + f"tok{sub}"])
                else:
                    s3 = t % 3
                    P.dma("sp", f"xin{s3}", lambda e, s3=s3, t=t: e.dma_start(out=xT[s3][:].rearrange("p a b -> p (a b)"), in_=src[t]),
                          reads=[f"x2s{t}"], writes=[xr(t)])

            def transp_in(t):
                if t >= ntl or which != 0:
                    return
                s3 = t % 3
                for sub in range(2):
                    for half in range(2):
                        pb = half
                        trs = [(ptr[pb][:, jj * 128:(jj + 1) * 128], tok[sub][:, (half * 4 + jj) * 128:(half * 4 + jj + 1) * 128],
                                self.identf[:]) for jj in range(4)]
                        self.pe_transposes(trs, reads=[fp + f"tok{sub}", "identf"], writes=[fp + f"ptr{pb}"])
                        P.op("dve", lambda e, s3=s3, half=half, sub=sub, pb=pb: e.tensor_copy(
                            out=xT[s3][:, half * 4:(half + 1) * 4, sub * 128:(sub + 1) * 128],
                            in_=ptr[pb][:].rearrange("p (a b) -> p a b", b=128)),
                            reads=[fp + f"ptr{pb}"], writes=[xr(t)])

            def squares(t):
                if t >= ntl:
                    return
                s3 = t % 3
                for kc in range(8):
                    P.op("act", lambda e, kc=kc, s3=s3: e.activation(out=sq8[:, kc, :], in_=xT[s3][:, kc, :], func=AF.Square),
                         reads=[xr(t)], writes=[fp + f"sq8_{kc}"])

            def stat_mm(t):
                if t >= ntl:
                    return
                for kc in range(8):
                    self.pe_group([(st[:, 0:TT], self.onesb[:], sq8[:, kc, :], kc == 0, kc == 7)],
                                  reads=[fp + f"sq8_{kc}", "onesb"], writes=[fp + "st"])

            def norm_apply(t, step=None):
                if t >= ntl:
                    return
                s3, s2, cd = t % 3, t % 2, cdof(t)
                if step is None or step == 0:
                    P.op("act", lambda e: e.activation(out=t1c[:], in_=st[:, 0:TT], func=AF.Ln, scale=1.0 / D, bias=self.epsc[:, 0:1]),
                         reads=[fp + "st", "epsc"], writes=[fp + "t1a"])
                    P.op("act", lambda e: e.activation(out=rsa[:], in_=t1c[:], func=AF.Exp, scale=-0.5), reads=[fp + "t1a"], writes=[fp + "rsa"])
                kcs = range(8) if step is None else (range(0) if step == 0 else range((step - 1) * 2, step * 2))
                for kc in kcs:
                    b = kc % 2
                    P.op("dve", lambda e, kc=kc, b=b, s3=s3: e.tensor_tensor(out=tmpn[b][:], in0=xT[s3][:, kc, :], in1=rsa[:], op=ALU.mult),
                         reads=[xr(t), fp + "rsa"], writes=[fp + f"tmpn{b}"])
                    P.op("act", lambda e, kc=kc, b=b, s2=s2, cd=cd: e.activation(out=hT[s2][:, kc, :], in_=tmpn[b][:], func=AF.Identity,
                                                                              scale=self.mod(0, j, kc, cd), bias=self.mod(1, j, kc, cd)),
                         reads=[fp + f"tmpn{b}", "mods"], writes=[hr(t)])

            def mm1(t, f0=0, f1=NFC, hook=None):
                s2 = t % 2
                for fc in range(f0, f1):
                    b = fc % 2
                    mms = [(pgu[b][:, 0:TT], W1g[:, kc, fc * 128:(fc + 1) * 128], hT[s2][:, kc, :], kc == 0, kc == 7) for kc in range(8)]
                    mms += [(pgu[b][:, TT:2 * TT], W1u[:, kc, fc * 128:(fc + 1) * 128], hT[s2][:, kc, :], kc == 0, kc == 7)
                            for kc in range(8)]
                    self.pe_group(mms, reads=w1res_for(fc) + [hr(t)], writes=[fp + f"pgu{b}"])
                    P.op("act", lambda e, b=b: e.activation(out=sg[b][:], in_=pgu[b][:, 0:TT], func=AF.Silu),
                         reads=[fp + f"pgu{b}"], writes=[fp + f"sg{b}"])
                    P.op("dve", lambda e, b=b, fc=fc: e.tensor_tensor(out=actT[:, fc, :], in0=sg[b][:], in1=pgu[b][:, TT:2 * TT], op=ALU.mult),
                         reads=[fp + f"sg{b}", fp + f"pgu{b}"], writes=[fp + f"actT{fc}"])
                    if hook is not None:
                        hook(fc)

            actres = [fp + f"actT{fc}" for fc in range(NFC)]

            def mm2(t, hook=None):
                def stat(dc):
                    b = dc % 2
                    self.pe_group([(st2[:, 0:TT], self.onesb[:], sqp[b][:], dc == 0, dc == 7)], reads=[fp + f"sqp{b}", "onesb"], writes=[fp + "st2"])
                for dc in range(8):
                    b = dc % 2
                    mms = [(po[b][:, 0:TT], W2[:, fc, dc * 128:(dc + 1) * 128], actT[:, fc, :], fc == 0, fc == NFC - 1) for fc in range(NFC)]
                    self.pe_group(mms, reads=w2res + actres, writes=[fp + f"po{b}"])
                    if dc >= 1:
                        stat(dc - 1)
                    P.op("dve", lambda e, dc=dc, b=b: e.tensor_copy(out=oT[:, dc, :], in_=po[b][:, 0:TT]),
                         reads=[fp + f"po{b}"], writes=[fp + f"oT{dc}"])
                    P.op("act", lambda e, dc=dc, b=b: e.activation(out=sqp[b][:], in_=oT[:, dc, :], func=AF.Square),
                         reads=[fp + f"oT{dc}"], writes=[fp + f"sqp{b}"])
                    if hook is not None:
                        hook(dc)
                stat(7)

            def post(t, step=None):
                s3, cd = t % 3, cdof(t)
                if step is None or step == 0:
                    P.op("act", lambda e: e.activation(out=t1b[:], in_=st2[:, 0:TT], func=AF.Ln, scale=1.0 / D, bias=self.epsc[:, 0:1]),
                         reads=[fp + "st2", "epsc"], writes=[fp + "t1a"])
                    P.op("act", lambda e: e.activation(out=rsb[:], in_=t1b[:], func=AF.Exp, scale=-0.5), reads=[fp + "t1a"], writes=[fp + "rsb"])
                dcs = range(8) if step is None else (range(0) if step == 0 else range(step - 1, step))
                for dc in dcs:
                    b = dc % 2
                    P.op("dve", lambda e, dc=dc, b=b, cd=cd: e.scalar_tensor_tensor(out=tmpf[b][:], in0=oT[:, dc, :], scalar=self.mod(2, j, dc, cd),
                                                                                     in1=rsb[:], op0=ALU.mult, op1=ALU.mult),
                         reads=[fp + f"oT{dc}", fp + "rsb", "mods"], writes=[fp + f"tmp{b}"])
                    P.op("dve", lambda e, dc=dc, b=b, s3=s3: e.tensor_tensor(out=xT[s3][:, dc, :], in0=xT[s3][:, dc, :], in1=tmpf[b][:], op=ALU.add),
                         reads=[xr(t), fp + f"tmp{b}"], writes=[xr(t)])

            def store(t):
                s3 = t % 3
                if which == 0:
                    P.dma("sp", f"xout{s3}", lambda e, s3=s3, t=t: e.dma_start(out=self.x1s[t], in_=xT[s3][:].rearrange("p a b -> p (a b)")),
                          reads=[xr(t)], writes=[f"x1s{t}"])
                    if self.debug:
                        P.dma("sp", f"dbgx{s3}", lambda e, s3=s3, t=t: e.dma_start(out=self.dbg_x1[t], in_=xT[s3][:].rearrange("p a b -> p (a b)")),
                              reads=[xr(t)])
                else:
                    for sub in range(2):
                        r0 = t * TT + sub * 128
                        for half in range(2):
                            pb = half
                            trs = [(ptr[pb][:, jj * 128:(jj + 1) * 128], xT[s3][:, half * 4 + jj, sub * 128:(sub + 1) * 128],
                                    self.identf[:]) for jj in range(4)]
                            self.pe_transposes(trs, reads=[xr(t), "identf"], writes=[fp + f"ptr{pb}"])
                            P.op("act", lambda e, half=half, sub=sub, pb=pb: e.copy(
                                out=tok[sub][:, half * 512:(half + 1) * 512], in_=ptr[pb][:]),
                                reads=[fp + f"ptr{pb}"], writes=[fp + f"tok{sub}"])
                        P.dma("sp", f"tok{sub}", lambda e, sub=sub, r0=r0: e.dma_start(out=self.y_all[r0:r0 + 128, :], in_=tok[sub][:]),
                              reads=[fp + f"tok{sub}"])

            load(0)
            transp_in(0)
            if which == 0:
                load(1)
            else:
                load(1)
            squares(0)
            stat_mm(0)
            norm_apply(0)
            for t in range(ntl):
                def h1(fc, t=t):
                    if t >= 1 and fc < 8:
                        post(t - 1, fc + 1)
                mm1(t, 0, 4, hook=h1)
                transp_in(t + 1)
                if which == 0:
                    load(t + 2)
                squares(t + 1)
                mm1(t, 4, NFC, hook=h1)
                if t >= 1:
                    store(t - 1)
                if which == 1:
                    load(t + 2)
                stat_mm(t + 1)
                mm2(t, hook=lambda dc, t=t: norm_apply(t + 1, dc) if dc <= 4 else None)
                post(t, 0)
            for k in range(8):
                post(ntl - 1, k + 1)
            store(ntl - 1)
            P.barrier()
            P.emit()

    def phase_mixer(self, grp):
        P = self.P
        nc = self.nc
        sample = grp == 0
        tiles = list(range(0, NTS)) if sample else list(range(NTS, NT))
        ntl = len(tiles)
        ntok = ntl * TT
        tok_base = tiles[0] * TT
        nseq = 1 if sample else 4
        L = LS if sample else LP
        nkt_new = ntok // 128
        nkt = nkt_new + (PAST // 128 if sample else 0)
        nkeys = nkt * 128
        cd = 0 if sample else 1
        g = f"g{grp}"
        with ExitStack() as mx:
            attnT = self.T(mx, g + "attnT", [128, 4, ntok], BF16)
            with ExitStack() as ph:
                b1 = ExitStack()
                QT = self.T(ph, g + "QT", [128, 4, ntok], BF16)
                KT = self.T(ph, g + "KT", [128, 2, nkeys], BF16)
                VA = self.T(ph, g + "VA", [128, nkt, 2, 192], BF16)
                Win = self.T(b1, g + "Win", [128, 8, 1792], BF16)
                gv = self.T(b1, g + "gv", [128, 640], F32)
                xT = [self.T(b1, g + f"xT{i}", [128, 8, TT], F32) for i in range(2)]
                hT = [self.T(b1, g + f"hT{i}", [128, 8, TT], BF16) for i in range(2)]
                qk = [self.T(b1, g + f"qk{i}", [128, 640], F32) for i in range(2)]
                sqq = self.T(b1, g + "sqq", [128, 640], F32)
                qn = qk
                ssq = self.T(b1, g + "ssq", [128, 20], F32)
                ssq2 = self.T(b1, g + "ssq2", [128, 20], F32)
                rs = self.T(b1, g + "rs", [128, 20], F32)
                sq8 = self.T(b1, g + "sq8", [128, 8, TT], BF16)
                cs = [self.T(b1, g + f"cs{i}", [128, 128], F32) for i in range(2)] if sample else None
                r1 = sqq
                r2 = self.T(b1, g + "r2", [128, 640], F32) if sample else None
                qr = [self.T(b1, g + f"qr{i}", [128, 640], BF16) for i in range(4)]
                kdup = [self.T(b1, g + f"kdup{i}", [128, 256], BF16) for i in range(4)]
                vf = [self.T(b1, g + f"vf{i}", [128, 128], F32) for i in range(2)] if not sample else None
                xlt = [self.T(b1, g + f"xlt{i}", [128, 4, TT], F32) for i in range(1)]
                gh = self.T(b1, g + "gh", [128, 4 * TT], F32)
                gx2 = self.T(b1, g + "gx2", [128, 4 * TT], F32)
                gth = gx2
                ggt = [gx2[:].rearrange("p (a b) -> p a b", a=4)]
                ck = gh[:, 0:512].rearrange("p (a f) -> p a f", a=4)
                cv = gh[:, 512:1024].rearrange("p (a f) -> p a f", a=4)
                bufs = dict(
                    t1=self.T(b1, g + "t1", [128, TT], F32), rstd=self.T(b1, g + "rstd", [128, TT], F32),
                    tmp=[self.T(b1, g + f"tmp{i}", [128, TT], F32) for i in range(2)])
                with ExitStack() as pp:
                    bufs["st"] = self.PS(pp, g + "st", [128, 512], F32)
                    pq = [self.PS(pp, g + f"pq{i}", [128, 512], F32) for i in range(2)]
                    pkv = [self.PS(pp, g + f"pkv{i}", [128, 512], F32) for i in range(2)]
                    pxg = [self.PS(pp, g + f"pxg{i}", [128, 512], F32) for i in range(2)]
                    ptqk = self.PS(pp, g + "ptqk", [128, 1024], BF16)
                    ptq = ptqk[:, 0:512]
                    ptk = ptqk[:, 512:768]
                    winv = self.w_in.rearrange("(k p) f -> p k f", p=128)
                    for wi, (c0, c1) in enumerate(((0, 768), (768, 1280), (1280, 1792))):
                        P.dma("pool", f"win{wi}", lambda e, c0=c0, c1=c1: e.dma_start(out=Win[:, :, c0:c1], in_=winv[:, :, c0:c1]),
                              writes=[g + f"Win{wi}"])
                    winres = [g + "Win0"]
                    winres_x = [g + "Win1", g + "Win2"]
                    if getattr(self, "after_win_hook", None) is not None:
                        self.after_win_hook()
                    gvb = AP(self.gvec.tensor, 0, [[0, 128], [1, 640]])
                    P.dma("sp", "gv", lambda e: e.dma_start(out=gv[:], in_=gvb), writes=[g + "gv"])
                    P.op("pool", lambda e: e.memset(VA[:, :, :, 64:128], 1.0), writes=[g + "VA"])
                    if sample:
                        P.dma("sp", "ck", lambda e: e.dma_start(out=ck[:], in_=self.cache_k.rearrange("(a p) f -> p a f", p=128)),
                              writes=[g + "gh"])
                        P.dma("sp", "cv", lambda e: e.dma_start(out=cv[:], in_=self.cache_v.rearrange("(a p) f -> p a f", p=128)),
                              writes=[g + "gh"])
                        for a in range(4):
                            kt = nkt_new + a
                            b = a % 2
                            P.op("dve", lambda e, a=a, b=b: e.tensor_copy(
                                out=kdup[b][:].rearrange("p (k d f) -> p k d f", k=2, d=2),
                                in_=cust(ck[:, a, :], [[64, 2], [0, 2], [1, 64]])),
                                reads=[g + "gh"], writes=[g + f"kdup{b}"])
                            self.pe_transposes([(ptk[:, 0:128], kdup[b][:, 0:128], self.identb[:]),
                                                (ptk[:, 128:256], kdup[b][:, 128:256], self.identb[:])],
                                               reads=[g + f"kdup{b}", "identb"], writes=[g + "ptqk"])
                            P.op("dve", lambda e, kt=kt: e.tensor_copy(out=KT[:, :, kt * 128:(kt + 1) * 128],
                                                                       in_=ptk[:].rearrange("p (k t) -> p k t", k=2)),
                                 reads=[g + "ptqk"], writes=[g + "KT"])
                            for off in (0, 128):
                                P.op("dve", lambda e, a=a, kt=kt, off=off: e.tensor_copy(
                                    out=VA[:, kt, :, off:off + 64], in_=cv[:, a, :].rearrange("p (k f) -> p k f", k=2)),
                                    reads=[g + "gh"], writes=[g + "VA"])
                    lvl = getattr(self, "lvl", 99)
                    ntl_ = min(ntl, getattr(self, "ntl", 99))
                    st = bufs["st"]
                    t1, rstd, tmp = bufs["t1"], bufs["rstd"], bufs["tmp"]

                    def b1_load(ti):
                        if ti >= ntl_:
                            return
                        t, s2 = tiles[ti], ti % 2
                        P.dma("sp", f"xin{s2}", lambda e, s2=s2, t=t: e.dma_start(out=xT[s2][:].rearrange("p a b -> p (a b)"), in_=self.x1s[t]),
                              reads=[f"x1s{t}"], writes=[g + f"mxT{s2}"])

                    def b1_squares(ti):
                        if ti >= ntl_:
                            return
                        s2 = ti % 2
                        for kc in range(8):
                            P.op("act", lambda e, kc=kc, s2=s2: e.activation(out=sq8[:, kc, :], in_=xT[s2][:, kc, :], func=AF.Square),
                                 reads=[g + f"mxT{s2}"], writes=[g + f"sq8_{kc}"])

                    def b1_stat(ti):
                        if ti >= ntl_:
                            return
                        for kc in range(8):
                            self.pe_group([(st[:, 0:TT], self.onesb[:], sq8[:, kc, :], kc == 0, kc == 7)],
                                          reads=[g + f"sq8_{kc}", "onesb"], writes=[g + "st"])

                    def b1_apply(ti):
                        if ti >= ntl_:
                            return
                        s2 = ti % 2
                        P.op("act", lambda e: e.activation(out=t1[:], in_=st[:, 0:TT], func=AF.Ln, scale=1.0 / D, bias=self.epsc[:, 0:1]),
                             reads=[g + "st", "epsc"], writes=[g + "t1"])
                        P.op("act", lambda e: e.activation(out=rstd[:], in_=t1[:], func=AF.Exp, scale=-0.5), reads=[g + "t1"], writes=[g + "rstd"])
                        for kc in range(8):
                            b = kc % 2
                            P.op("dve", lambda e, kc=kc, b=b, s2=s2: e.tensor_tensor(out=tmp[b][:], in0=xT[s2][:, kc, :], in1=rstd[:], op=ALU.mult),
                                 reads=[g + f"mxT{s2}", g + "rstd"], writes=[g + f"tmp{b}"])
                            P.op("act", lambda e, kc=kc, b=b, s2=s2: e.activation(out=hT[s2][:, kc, :], in_=tmp[b][:], func=AF.Identity,
                                                                                  scale=self.mod(0, 1, kc, cd), bias=self.mod(1, 1, kc, cd)),
                                 reads=[g + f"tmp{b}", "mods"], writes=[g + f"mhT{s2}"])


                    def b1_transposes(ti):
                        for sb in range(2):
                            lt0 = ti * TT + sb * 128
                            kt = lt0 // 128
                            qb = (ti % 2) * 2 + sb
                            trs = [(ptq[:, jj * 128:(jj + 1) * 128], qr[qb][:, jj * 128:(jj + 1) * 128], self.identb[:]) for jj in range(4)]
                            trs += [(ptk[:, 0:128], kdup[qb][:, 0:128], self.identb[:]), (ptk[:, 128:256], kdup[qb][:, 128:256], self.identb[:])]
                            self.pe_transposes(trs, reads=[g + f"qr{qb}", g + f"kdup{qb}", "identb"], writes=[g + "ptqk"])
                            P.op("dve", lambda e, lt0=lt0: e.tensor_copy(out=QT[:, :, lt0:lt0 + 128],
                                                                         in_=ptq[:].rearrange("p (a t) -> p a t", a=4)),
                                 reads=[g + "ptqk"], writes=[g + "QT"])
                            P.op("dve", lambda e, kt=kt: e.tensor_copy(out=KT[:, :, kt * 128:(kt + 1) * 128],
                                                                       in_=ptk[:].rearrange("p (k t) -> p k t", k=2)),
                                 reads=[g + "ptqk"], writes=[g + "KT"])

                    b1_load(0)
                    b1_load(1)
                    b1_squares(0)
                    b1_stat(0)
                    b1_apply(0)
                    for ti in range(ntl_):
                        t = tiles[ti]
                        p = ti % 2
                        hres = g + f"mhT{p}"
                        b1_squares(ti + 1)
                        b1_load(ti + 2)
                        for sb in range(2):
                            lt0 = ti * TT + sb * 128
                            kt = lt0 // 128
                            hsl = lambda kc: hT[p][:, kc, sb * 128:(sb + 1) * 128]
                            self.pe_group([(pq[sb][:], hsl(kc), Win[:, kc, 0:512], kc == 0, kc == 7) for kc in range(8)],
                                          reads=winres + [hres], writes=[g + f"pq{sb}"])
                            self.pe_group([(pkv[sb][:, 0:256], hsl(kc), Win[:, kc, 512:768], kc == 0, kc == 7) for kc in range(8)],
                                          reads=winres + [hres], writes=[g + f"pkv{sb}"])
                            P.op("act", lambda e, sb=sb: e.copy(out=qk[sb][:, 0:512], in_=pq[sb][:]),
                                 reads=[g + f"pq{sb}"], writes=[g + f"qk{sb}"])
                            P.op("act", lambda e, sb=sb: e.copy(out=qk[sb][:, 512:640], in_=pkv[sb][:, 0:128]),
                                 reads=[g + f"pkv{sb}"], writes=[g + f"qk{sb}"])
                            for off in (0, 128):
                                P.op("act", lambda e, sb=sb, kt=kt, off=off: e.copy(
                                    out=VA[:, kt, :, off:off + 64], in_=pkv[sb][:, 128:256].rearrange("p (k f) -> p k f", k=2)),
                                    reads=[g + f"pkv{sb}"], writes=[g + "VA"])
                            if not sample:
                                P.op("act", lambda e, sb=sb: e.copy(out=vf[sb][:], in_=pkv[sb][:, 128:256]),
                                     reads=[g + f"pkv{sb}"], writes=[g + f"vf{sb}"])
                                P.dma("sp", f"vf{sb}", lambda e, sb=sb, lt0=lt0: e.dma_start(out=self.ncv[lt0:lt0 + 128, :], in_=vf[sb][:]),
                                      reads=[g + f"vf{sb}"])
                        b1_stat(ti + 1)
                        b1_apply(ti + 1)
                        for ch in range(8):
                            hb = ch % 2
                            self.pe_group([(pxg[hb][:, 0:TT], Win[:, kc, 768 + ch * 128:768 + (ch + 1) * 128], hT[p][:, kc, :],
                                            kc == 0, kc == 7) for kc in range(8)], reads=winres_x + [hres], writes=[g + f"pxg{hb}"])
                            if ch < 4:
                                P.op("act", lambda e, ch=ch, hb=hb: e.copy(out=xlt[0][:, ch, :], in_=pxg[hb][:, 0:TT]),
                                     reads=[g + f"pxg{hb}"], writes=[g + "xlt0"])
                            else:
                                P.op("act", lambda e, ch=ch, hb=hb: e.copy(out=gh[:, (ch - 4) * TT:(ch - 3) * TT], in_=pxg[hb][:, 0:TT]),
                                     reads=[g + f"pxg{hb}"], writes=[g + "gh"])
                        if ti >= 1:
                            b1_transposes(ti - 1)
                        for sb in range(2):
                            P.op("dve", lambda e, sb=sb: e.tensor_tensor(out=sqq[:], in0=qk[sb][:], in1=qk[sb][:], op=ALU.mult),
                                 reads=[g + f"qk{sb}"], writes=[g + "sqq"])
                            P.op("dve", lambda e, sb=sb: e.tensor_reduce(out=ssq[:, sb * 10:sb * 10 + 10], in_=sqq[:].rearrange("p (h f) -> p h f", f=64),
                                                                         axis=AX.X, op=ALU.add),
                                 reads=[g + "sqq"], writes=[g + f"ssq{sb}"])
                        P.op("act", lambda e: e.activation(out=ssq2[:], in_=ssq[:], func=AF.Ln, scale=1.0 / 64, bias=self.epsc[:, 0:1]),
                             reads=[g + "ssq0", g + "ssq1", "epsc"], writes=[g + "ssq2"])
                        P.op("act", lambda e: e.activation(out=rs[:], in_=ssq2[:], func=AF.Exp, scale=-0.5),
                             reads=[g + "ssq2"], writes=[g + "rs"])
                        P.op("pool", lambda e: e.tensor_tensor(out=gx2[:], in0=gh[:], in1=gh[:], op=ALU.mult), reads=[g + "gh"], writes=[g + "gx2"])
                        P.op("pool", lambda e: e.tensor_scalar(out=gx2[:], in0=gx2[:], scalar1=0.044715, scalar2=1.0, op0=ALU.mult, op1=ALU.add),
                             reads=[g + "gx2"], writes=[g + "gx2"])
                        P.op("pool", lambda e: e.tensor_tensor(out=gx2[:], in0=gx2[:], in1=gh[:], op=ALU.mult),
                             reads=[g + "gx2", g + "gh"], writes=[g + "gx2"])
                        P.op("act", lambda e: e.activation(out=gth[:], in_=gx2[:], func=AF.Exp, scale=-2.0 * 0.7978845608028654),
                             reads=[g + "gx2"], writes=[g + "gx2"])
                        P.op("act", lambda e: e.activation(out=gth[:], in_=gth[:], func=AF.Ln, scale=1.0, bias=1.0),
                             reads=[g + "gx2"], writes=[g + "gx2"])
                        P.op("act", lambda e: e.activation(out=gth[:], in_=gth[:], func=AF.Exp, scale=-1.0),
                             reads=[g + "gx2"], writes=[g + "gx2"])
                        for sb in range(2):
                            lt0 = ti * TT + sb * 128
                            qb = (ti % 2) * 2 + sb
                            P.op("dve", lambda e, sb=sb: e.tensor_tensor(
                                out=qn[sb][:].rearrange("p (h f) -> p h f", f=64), in0=qk[sb][:].rearrange("p (h f) -> p h f", f=64),
                                in1=cust(rs[:, sb * 10:sb * 10 + 10], [[1, 10], [0, 64]]), op=ALU.mult),
                                reads=[g + f"qk{sb}", g + "rs"], writes=[g + f"qk{sb}"])
                            P.op("dve", lambda e, sb=sb: e.tensor_tensor(out=qn[sb][:], in0=qn[sb][:], in1=gv[:], op=ALU.mult),
                                 reads=[g + f"qk{sb}", g + "gv"], writes=[g + f"qk{sb}"])
                            if not sample:
                                P.dma("sp", f"kf{sb}", lambda e, sb=sb, lt0=lt0: e.dma_start(out=self.nck[lt0:lt0 + 128, :], in_=qn[sb][:, 512:640]),
                                      reads=[g + f"qk{sb}"])
                                P.op("dve", lambda e, sb=sb, qb=qb: e.tensor_copy(out=qr[qb][:], in_=qn[sb][:]),
                                     reads=[g + f"qk{sb}"], writes=[g + f"qr{qb}"])
                            else:
                                P.dma("sp", f"cs{sb}", lambda e, sb=sb, lt0=lt0: e.dma_start(out=cs[sb][:], in_=self.rope[lt0:lt0 + 128, :]),
                                      writes=[g + f"cs{sb}"])
                                P.op("dve", lambda e, sb=sb: e.tensor_tensor(
                                    out=r1[:].rearrange("p (h f) -> p h f", f=64), in0=qn[sb][:].rearrange("p (h f) -> p h f", f=64),
                                    in1=cust(cs[sb][:, 0:64], [[0, 10], [1, 64]]), op=ALU.mult),
                                    reads=[g + f"qk{sb}", g + f"cs{sb}"], writes=[g + "sqq"])
                                for h in range(2):
                                    P.op("dve", lambda e, sb=sb, h=h: e.tensor_tensor(
                                        out=cust(r2[:, h * 16:h * 16 + 16], [[64, 10], [32, 2], [1, 16]]),
                                        in0=cust(qn[sb][:, (1 - h) * 16:(1 - h) * 16 + 16], [[64, 10], [32, 2], [1, 16]]),
                                        in1=cust(cs[sb][:, 64 + h * 16:64 + h * 16 + 16], [[0, 10], [32, 2], [1, 16]]), op=ALU.mult),
                                        reads=[g + f"qk{sb}", g + f"cs{sb}"], writes=[g + "r2"])
                                P.op("dve", lambda e, qb=qb: e.tensor_tensor(out=qr[qb][:], in0=r1[:], in1=r2[:], op=ALU.add),
                                     reads=[g + "sqq", g + "r2"], writes=[g + f"qr{qb}"])
                            P.op("dve", lambda e, qb=qb: e.tensor_copy(
                                out=kdup[qb][:].rearrange("p (k d f) -> p k d f", k=2, d=2),
                                in_=cust(qr[qb][:, 512:640], [[64, 2], [0, 2], [1, 64]])),
                                reads=[g + f"qr{qb}"], writes=[g + f"kdup{qb}"])
                        P.op("pool", lambda e: e.tensor_tensor(out=gx2[:], in0=gth[:], in1=gh[:], op=ALU.mult),
                             reads=[g + "gx2", g + "gh"], writes=[g + "gx2"])
                        c0 = t * TT
                        P.dma("sp", f"xlo{p}", lambda e, c0=c0: e.dma_start(
                            out=self.xls.rearrange("c p t -> p c t")[:, :, c0:c0 + TT], in_=xlt[0][:]),
                            reads=[g + "xlt0"], writes=[f"xls{t}"])
                        P.dma("sp", f"ggo{p}", lambda e, c0=c0: e.dma_start(
                            out=self.ggs.rearrange("c p t -> p c t")[:, :, c0:c0 + TT], in_=ggt[0][:]),
                            reads=[g + "gx2"], writes=[f"ggs{t}"])
                    b1_transposes(ntl_ - 1)
                    P.barrier()
                    P.emit()
                b1.close()
                with ExitStack() as pp:
                    NQ = 512 if sample else 256
                    pS = [self.PS(pp, g + f"pS{i}", [128, 2, 512], F32) for i in range(2)]
                    pO = [[self.PS(pp, g + f"pO{i}{h}", [128, 512], F32) for h in range(2)] for i in range(2)]
                    pT = [self.T(pp, g + f"pT{i}", [128, 2, 512], BF16) for i in range(2)]
                    rec = [self.T(pp, g + f"rec{i}", [128, 512], F32) for i in range(2)]
                    blocks = []
                    if sample:
                        for qb in range(LS // 512):
                            blocks.append((qb * 512, list(range(nkt))))
                    else:
                        for s in range(4):
                            blocks.append((s * LP, [2 * s, 2 * s + 1]))
                    it = 0
                    if lvl < 17:
                        blocks = []
                    for (q0, kts) in blocks[:getattr(self, "nqb", 99)]:
                        for pair in range(4):
                            kv = pair // 2
                            ob = it % 2
                            it += 1

                            def qk_mm(i, kt):
                                sbk = i % 2
                                mms = [(pS[sbk][:, h, 0:NQ], KT[h * 64:(h + 1) * 64, kv, kt * 128:(kt + 1) * 128],
                                        QT[h * 64:(h + 1) * 64, pair, q0:q0 + NQ], True, True) for h in range(2)]
                                self.pe_group(mms, reads=[g + "KT", g + "QT"], writes=[g + f"pS{sbk}"])
                            qk_mm(0, kts[0])
                            for i, kt in enumerate(kts):
                                sbk = i % 2
                                if i + 1 < len(kts):
                                    qk_mm(i + 1, kts[i + 1])
                                P.op("act", lambda e, sbk=sbk: e.activation(out=pT[sbk][:, :, 0:NQ], in_=pS[sbk][:, :, 0:NQ],
                                                                            func=AF.Exp, scale=0.125),
                                     reads=[g + f"pS{sbk}"], writes=[g + f"pT{sbk}"])
                                mms = [(pO[ob][h][:, 0:NQ], VA[:, kt, kv, h * 64:h * 64 + 128], pT[sbk][:, h, 0:NQ],
                                        i == 0, i == len(kts) - 1) for h in range(2)]
                                self.pe_group(mms, reads=[g + "VA", g + f"pT{sbk}"], writes=[g + f"pO{ob}0", g + f"pO{ob}1"])
                            P.op("dve", lambda e, ob=ob: e.reciprocal(out=rec[0][0:64, 0:NQ], in_=pO[ob][0][64:128, 0:NQ]),
                                 reads=[g + f"pO{ob}0"], writes=[g + "rec0"])
                            P.op("dve", lambda e, ob=ob, pair=pair, q0=q0: e.tensor_tensor(
                                out=attnT[0:64, pair, q0:q0 + NQ], in0=pO[ob][0][0:64, 0:NQ], in1=rec[0][0:64, 0:NQ], op=ALU.mult),
                                reads=[g + f"pO{ob}0", g + "rec0"], writes=[g + "attnT"])
                            P.op("dve", lambda e, ob=ob: e.reciprocal(out=rec[1][64:128, 0:NQ], in_=pO[ob][1][0:64, 0:NQ]),
                                 reads=[g + f"pO{ob}1"], writes=[g + "rec1"])
                            P.op("dve", lambda e, ob=ob, pair=pair, q0=q0: e.tensor_tensor(
                                out=attnT[64:128, pair, q0:q0 + NQ], in0=pO[ob][1][64:128, 0:NQ], in1=rec[1][64:128, 0:NQ], op=ALU.mult),
                                reads=[g + f"pO{ob}1", g + "rec1"], writes=[g + "attnT"])
                    P.barrier()
                    P.emit()
            lruT = self.T(mx, g + "lruT", [128, 4, ntok], BF16)
            if lvl < 18:
                return
            self.lru(grp, mx, lruT, tiles, nseq, L, tok_base, ntok)
            if lvl < 19:
                return
            with ExitStack() as ph:
                Wo = self.T(ph, g + "Wo", [128, 8, D], BF16)
                xT = [self.T(ph, g + f"oxT{i}", [128, 8, TT], F32) for i in range(3)]
                oT = self.T(ph, g + "ooT", [128, 8, TT], F32)
                sq = [self.T(ph, g + f"osq{i}", [128, TT], BF16) for i in range(2)]
                t1 = self.T(ph, g + "ot1", [128, TT], F32)
                rstd = self.T(ph, g + "orstd", [128, TT], F32)
                tmpf = [self.T(ph, g + f"otmpf{i}", [128, TT], F32) for i in range(2)]
                st2 = self.PS(ph, g + "ost2", [128, 512], F32)
                po = [self.PS(ph, g + f"opo{i}", [128, 512], F32) for i in range(2)]
                wov = self.w_out.rearrange("(k p) f -> p k f", p=128)
                for kc in range(8):
                    P.dma("pool", f"win{kc}", lambda e, kc=kc: e.dma_start(out=Wo[:, kc, :], in_=wov[:, kc, :]), writes=[g + f"Wo{kc}"])
                wores = [g + f"Wo{kc}" for kc in range(8)]
                xr = lambda ti: g + f"oxT{ti % 3}"

                def o_load(ti):
                    if ti >= ntl:
                        return
                    t, s3 = tiles[ti], ti % 3
                    P.dma("sp", f"xin{s3}", lambda e, s3=s3, t=t: e.dma_start(out=xT[s3][:].rearrange("p a b -> p (a b)"), in_=self.x1s[t]),
                          reads=[f"x1s{t}"], writes=[xr(ti)])

                def o_stat(dc):
                    b = dc % 2
                    self.pe_group([(st2[:, 0:TT], self.onesb[:], sq[b][:], dc == 0, dc == 7)], reads=[g + f"osq{b}", "onesb"], writes=[g + "ost2"])

                def o_post_step(ti, dc):
                    s3, b = ti % 3, dc % 2
                    P.op("dve", lambda e, dc=dc, b=b: e.scalar_tensor_tensor(out=tmpf[b][:], in0=oT[:, dc, :], scalar=self.mod(2, 1, dc, cd),
                                                                             in1=rstd[:], op0=ALU.mult, op1=ALU.mult),
                         reads=[g + f"ooT{dc}", g + "orstd", "mods"], writes=[g + f"otmpf{b}"])
                    P.op("dve", lambda e, dc=dc, b=b, s3=s3: e.tensor_tensor(out=xT[s3][:, dc, :], in0=xT[s3][:, dc, :], in1=tmpf[b][:], op=ALU.add),
                         reads=[xr(ti), g + f"otmpf{b}"], writes=[xr(ti)])

                def o_store(ti):
                    t, s3 = tiles[ti], ti % 3
                    P.dma("sp", f"xout{s3}", lambda e, s3=s3, t=t: e.dma_start(out=self.x2s[t], in_=xT[s3][:].rearrange("p a b -> p (a b)")),
                          reads=[xr(ti)], writes=[f"x2s{t}"])
                    if self.debug:
                        P.dma("sp", f"dbgx{s3}", lambda e, s3=s3, t=t: e.dma_start(out=self.dbg_x2[t], in_=xT[s3][:].rearrange("p a b -> p (a b)")),
                              reads=[xr(ti)])

                o_load(0)
                o_load(1)
                for ti in range(ntl):
                    l0 = ti * TT
                    for dc in range(8):
                        b = dc % 2
                        mms = []
                        for kc in range(8):
                            src = attnT[:, kc, l0:l0 + TT] if kc < 4 else lruT[:, kc - 4, l0:l0 + TT]
                            mms.append((po[b][:, 0:TT], Wo[:, kc, dc * 128:(dc + 1) * 128], src, kc == 0, kc == 7))
                        self.pe_group(mms, reads=wores + [g + "attnT", g + "lruT"], writes=[g + f"opo{b}"])
                        if dc >= 1:
                            o_stat(dc - 1)
                        if ti >= 1:
                            o_post_step(ti - 1, dc)
                        P.op("dve", lambda e, dc=dc, b=b: e.tensor_copy(out=oT[:, dc, :], in_=po[b][:, 0:TT]),
                             reads=[g + f"opo{b}"], writes=[g + f"ooT{dc}"])
                        P.op("act", lambda e, dc=dc, b=b: e.activation(out=sq[b][:], in_=oT[:, dc, :], func=AF.Square),
                             reads=[g + f"ooT{dc}"], writes=[g + f"osq{b}"])
                    o_stat(7)
                    if ti >= 1:
                        o_store(ti - 1)
                    o_load(ti + 2)
                    P.op("act", lambda e: e.activation(out=t1[:], in_=st2[:, 0:TT], func=AF.Ln, scale=1.0 / D, bias=self.epsc[:, 0:1]),
                         reads=[g + "ost2", "epsc"], writes=[g + "ot1"])
                    P.op("act", lambda e: e.activation(out=rstd[:], in_=t1[:], func=AF.Exp, scale=-0.5), reads=[g + "ot1"], writes=[g + "orstd"])
                for dc in range(8):
                    o_post_step(ntl - 1, dc)
                o_store(ntl - 1)
                P.barrier()
                P.emit()

    def lru(self, grp, mx, lruT, tiles, nseq, L, tok_base, ntok):
        P = self.P
        par = self.par
        sample = grp == 0
        g = f"l{grp}"
        SEG = 512
        nseg = max(1, L // SEG)
        with ExitStack() as ph:
            bdf = self.T(ph, g + "bdf", [128, 16 * 128], F32)
            bdb = self.T(ph, g + "bdb", [128, 16, 128], BF16)
            xlps = [self.T(ph, g + f"xlp{i}", [128, nseq, L + 4], F32) for i in range(2)]
            xc = self.T(ph, g + "xc", [128, nseq, L], F32)
            xcb = self.T(ph, g + "xcb", [128, nseq, L], BF16)
            ggc = self.T(ph, g + "ggc", [128, nseq, L], F32)
            hf = self.T(ph, g + "hf", [128, nseq, L], F32)
            spx = self.T(ph, g + "spx", [128, 32], F32)
            nb = self.T(ph, g + "nb", [128, 16], F32)
            hfin = self.T(ph, g + "hfin", [128, 32], F32)
            hfo = self.T(ph, g + "hfo", [32, 128], F32)
            eri = [self.T(ph, g + f"eri{i}", [128, 2, SEG], F32) for i in range(2)]
            ri = [self.T(ph, g + f"ri{i}", [128, 2, SEG], F32) for i in range(2)]
            av = [self.T(ph, g + f"av{i}", [128, SEG], F32) for i in range(2)]
            a2 = [self.T(ph, g + f"a2{i}", [128, SEG], F32) for i in range(2)]
            bxv = [self.T(ph, g + f"bxv{i}", [128, SEG], F32) for i in range(2)]
            bv = [self.T(ph, g + f"bv{i}", [128, SEG], F32) for i in range(2)]
            hb = [self.T(ph, g + f"hb{i}", [128, SEG], F32) for i in range(2)]
            yv = [self.T(ph, g + f"yv{i}", [128, SEG], F32) for i in range(2)]
            pz = [self.PS(ph, g + f"pz{i}", [128, 2, 512], F32) for i in range(2)]
            pfin = self.PS(ph, g + "pfin", [128, 128], F32)
            P.dma("sp", "bd", lambda e: e.dma_start(out=bdf[:], in_=self.bd), writes=[g + "bdf"])
            P.op("dve", lambda e: e.tensor_copy(out=bdb[:].rearrange("p a b -> p (a b)"), in_=bdf[:]), reads=[g + "bdf"], writes=[g + "bdb"])
            P.op("act", lambda e: e.activation(out=spx[:, 0:8], in_=par[:, C_LAM:C_LAM + 8], func=AF.Exp, scale=-1.0),
                 reads=["par"], writes=[g + "spx"])
            P.op("act", lambda e: e.activation(out=spx[:, 8:16], in_=spx[:, 0:8], func=AF.Ln, bias=1.0, scale=1.0),
                 reads=[g + "spx"], writes=[g + "spx"])
            P.op("dve", lambda e: e.tensor_scalar(out=spx[:, 16:24], in0=spx[:, 8:16], scalar1=-8.0, scalar2=None, op0=ALU.mult),
                 reads=[g + "spx"], writes=[g + "spx"])
            P.op("dve", lambda e: e.tensor_scalar(out=spx[:, 24:32], in0=spx[:, 8:16], scalar1=-16.0, scalar2=None, op0=ALU.mult),
                 reads=[g + "spx"], writes=[g + "spx"])
            P.op("dve", lambda e: e.tensor_scalar(out=nb[:], in0=par[:, C_BA:C_BA + 16], scalar1=-1.0, scalar2=None, op0=ALU.mult),
                 reads=["par"], writes=[g + "nb"])
            for i in range(2):
                P.op("pool", lambda e, i=i: e.memset(xlps[i][:, :, 0:2], 0.0), writes=[g + f"xlp{i}"])
                P.op("pool", lambda e, i=i: e.memset(xlps[i][:, :, L + 2:L + 4], 0.0), writes=[g + f"xlp{i}"])
            for c in range(4):
                xlp = xlps[c % 2]
                xres = g + f"xlp{c % 2}"

                def xl_load(cc):
                    if cc >= 4:
                        return
                    P.dma("sp", f"xlp{cc % 2}", lambda e, cc=cc: e.dma_start(
                        out=xlps[cc % 2][:, :, 2:2 + L], in_=self.xls[cc][:, tok_base:tok_base + ntok].rearrange("p (s t) -> p s t", s=nseq)),
                        reads=[f"xls{t}" for t in tiles], writes=[g + f"xlp{cc % 2}"])
                if c == 0:
                    xl_load(0)
                xl_load(c + 1)
                P.dma("sp", "ggc", lambda e, c=c: e.dma_start(
                    out=ggc[:], in_=self.ggs[c][:, tok_base:tok_base + ntok].rearrange("p (s t) -> p s t", s=nseq)),
                    reads=[f"ggs{t}" for t in tiles], writes=[g + "ggc"])
                cwv = [par[:, C_CONVW + c * 4 + jj:C_CONVW + c * 4 + jj + 1] for jj in range(4)]
                cbv = par[:, C_CONVB + c:C_CONVB + c + 1]
                PW = 1024 if sample else 512
                npieces = (nseq * L) // PW

                def conv_piece(k, cwv=cwv, cbv=cbv, xlp=xlp, xres=xres):
                    if sample:
                        src = lambda jj: xlp[:, 0, k * PW + jj:k * PW + jj + PW]
                        dst, dstb = xc[:, 0, k * PW:(k + 1) * PW], xcb[:, 0, k * PW:(k + 1) * PW]
                    else:
                        src = lambda jj: xlp[:, 2 * k:2 * k + 2, jj:jj + LP]
                        dst, dstb = xc[:, 2 * k:2 * k + 2, :], xcb[:, 2 * k:2 * k + 2, :]
                    P.op("act", lambda e: e.activation(out=dst, in_=src(0), func=AF.Identity, scale=cwv[0], bias=cbv),
                         reads=[xres, "par"], writes=[g + f"xcp{k}"])
                    for jj in range(1, 4):
                        P.op("dve", lambda e, jj=jj: e.scalar_tensor_tensor(out=dst, in0=src(jj), scalar=cwv[jj], in1=dst, op0=ALU.mult, op1=ALU.add),
                             reads=[xres, g + f"xcp{k}", "par"], writes=[g + f"xcp{k}"])
                    P.op("pool", lambda e: e.tensor_copy(out=dstb, in_=dst), reads=[g + f"xcp{k}"], writes=[g + f"xcbp{k}"])
                xcf = xc[:].rearrange("p s t -> p (s t)")
                xcbf = xcb[:].rearrange("p s t -> p (s t)")
                hff = hf[:].rearrange("p s t -> p (s t)")
                ggcf = ggc[:].rearrange("p s t -> p (s t)")
                units = []
                if sample:
                    for d in range(2):
                        segs = list(range(nseg)) if d == 0 else list(range(nseg - 1, -1, -1))
                        for si, sg_ in enumerate(segs):
                            units.append((d, sg_ * SEG, [(0, 0, SEG)], si))
                else:
                    for sp in range(2):
                        for d in range(2):
                            units.append((d, sp * SEG, [(2 * sp, 0, LP), (2 * sp + 1, LP, LP)], 0))
                def stage(it, which_stage, c=c, units=units):
                    d, f0, parts, si = units[it]
                    b = it % 2
                    dcol = d * 4 + c
                    if which_stage == 2:
                        return stage2(it, d, f0, parts, si, b, dcol, c)
                    ia = (d * 2 + 0) * 4 + c
                    ix = (d * 2 + 1) * 4 + c
                    self.pe_group([(pz[b][:, 0, 0:SEG], bdb[:, ia, :], xcbf[:, f0:f0 + SEG], True, True),
                                   (pz[b][:, 1, 0:SEG], bdb[:, ix, :], xcbf[:, f0:f0 + SEG], True, True)],
                                  reads=[g + "bdb", g + f"xcbp{f0 // PW}"], writes=[g + f"pz{b}"])
                    P.op("act", lambda e, b=b, dcol=dcol: e.activation(out=eri[b][:, 0, :], in_=pz[b][:, 0, 0:SEG], func=AF.Exp,
                                                                     scale=-1.0, bias=nb[:, dcol:dcol + 1]),
                         reads=[g + f"pz{b}", g + "nb"], writes=[g + f"eri{b}"])
                    P.op("act", lambda e, b=b, dcol=dcol: e.activation(out=eri[b][:, 1, :], in_=pz[b][:, 1, 0:SEG], func=AF.Exp,
                                                                     scale=-1.0, bias=nb[:, 8 + dcol:8 + dcol + 1]),
                         reads=[g + f"pz{b}", g + "nb"], writes=[g + f"eri{b}"])
                    P.op("act", lambda e, b=b: e.activation(out=eri[b][:], in_=eri[b][:], func=AF.Ln, scale=1.0, bias=1.0),
                         reads=[g + f"eri{b}"], writes=[g + f"eri{b}"])
                    P.op("act", lambda e, b=b: e.activation(out=ri[b][:], in_=eri[b][:], func=AF.Exp, scale=-1.0),
                         reads=[g + f"eri{b}"], writes=[g + f"ri{b}"])
                    P.op("act", lambda e, b=b, dcol=dcol: e.activation(out=av[b][:], in_=ri[b][:, 0, :], func=AF.Exp,
                                                                     scale=spx[:, 16 + dcol:16 + dcol + 1]),
                         reads=[g + f"ri{b}", g + "spx"], writes=[g + f"av{b}"])
                    P.op("dve", lambda e, b=b: e.tensor_tensor(out=a2[b][:], in0=av[b][:], in1=av[b][:], op=ALU.mult),
                         reads=[g + f"av{b}"], writes=[g + f"a2{b}"])
                    P.op("dve", lambda e, b=b, f0=f0: e.tensor_tensor(out=bxv[b][:], in0=ri[b][:, 1, :], in1=xcf[:, f0:f0 + SEG], op=ALU.mult),
                         reads=[g + f"ri{b}", g + f"xcp{f0 // PW}"], writes=[g + f"bxv{b}"])

                def stage2(it, d, f0, parts, si, b, dcol, c):
                    P.op("act", lambda e, b=b: e.activation(out=a2[b][:], in_=a2[b][:], func=AF.Ln, scale=-1.0, bias=1.0),
                         reads=[g + f"a2{b}"], writes=[g + f"a2{b}"])
                    P.op("act", lambda e, b=b: e.activation(out=a2[b][:], in_=a2[b][:], func=AF.Exp, scale=0.5),
                         reads=[g + f"a2{b}"], writes=[g + f"a2{b}"])
                    P.op("dve", lambda e, b=b: e.tensor_tensor(out=bv[b][:], in0=bxv[b][:], in1=a2[b][:], op=ALU.mult),
                         reads=[g + f"bxv{b}", g + f"a2{b}"], writes=[g + f"bv{b}"])
                    for (s, off, plen) in parts:
                        if sample and si == 0:
                            init, irs = par[:, C_H0 + dcol:C_H0 + dcol + 1], ["par"]
                        elif sample:
                            init, irs = (hff[:, f0 - 1:f0], [g + "hf"]) if d == 0 else (hb[1 - b][:, 0:1], [g + f"hb{1 - b}"])
                        else:
                            init, irs = 0.0, []
                        if d == 0:
                            P.op("dve", lambda e, b=b, f0=f0, off=off, plen=plen, init=init: e.tensor_tensor_scan(
                                out=hff[:, f0 + off:f0 + off + plen], data0=av[b][:, off:off + plen], data1=bv[b][:, off:off + plen],
                                initial=init, op0=ALU.mult, op1=ALU.add),
                                reads=[g + f"av{b}", g + f"bv{b}", g + "hf"] + irs, writes=[g + "hf"])
                            if not sample:
                                P.op("pool", lambda e, s=s, dcol=dcol, f0=f0, off=off, plen=plen: e.tensor_copy(
                                    out=hfin[:, s * 8 + dcol:s * 8 + dcol + 1], in_=hff[:, f0 + off + plen - 1:f0 + off + plen]),
                                    reads=[g + "hf"], writes=[g + "hfin"])
                        else:
                            rev = lambda tl, off=off, plen=plen: cust(tl[:, off + plen - 1:off + plen], [[-1, plen]])
                            P.op("dve", lambda e, b=b, init=init, rev=rev: e.tensor_tensor_scan(
                                out=rev(hb[b]), data0=rev(av[b]), data1=rev(bv[b]), initial=init, op0=ALU.mult, op1=ALU.add),
                                reads=[g + f"av{b}", g + f"bv{b}"] + irs, writes=[g + f"hb{b}"])
                            if not sample:
                                P.op("pool", lambda e, s=s, dcol=dcol, b=b, off=off: e.tensor_copy(
                                    out=hfin[:, s * 8 + dcol:s * 8 + dcol + 1], in_=hb[b][:, off:off + 1]),
                                    reads=[g + f"hb{b}"], writes=[g + "hfin"])
                    if d == 1:
                        P.op("pool", lambda e, b=b, f0=f0: e.tensor_tensor(out=yv[b][:], in0=hb[b][:], in1=hff[:, f0:f0 + SEG], op=ALU.add),
                             reads=[g + f"hb{b}", g + "hf"], writes=[g + f"yv{b}"])
                        P.op("dve", lambda e, b=b, f0=f0, c=c: e.tensor_tensor(
                            out=lruT[:, c, f0:f0 + SEG], in0=yv[b][:], in1=ggcf[:, f0:f0 + SEG], op=ALU.mult),
                            reads=[g + f"yv{b}", g + "ggc"], writes=[f"g{grp}lruT"])

                nextp = [1]

                def pieces_upto(j):
                    need = units[j][1] // PW + 1
                    while nextp[0] < npieces and nextp[0] <= need:
                        conv_piece(nextp[0])
                        nextp[0] += 1
                conv_piece(0)
                stage(0, 1)
                pieces_upto(0)
                for it in range(len(units)):
                    if it + 1 < len(units):
                        stage(it + 1, 1)
                        pieces_upto(it + 1)
                    stage(it, 2)
            if not sample:
                self.pe_transposes([(pfin[0:32, :], hfin[:, 0:32], self.identf[:])], reads=[g + "hfin", "identf"], writes=[g + "pfin"])
                P.op("act", lambda e: e.copy(out=hfo[:], in_=pfin[0:32, :]), reads=[g + "pfin"], writes=[g + "hfo"])
                P.dma("sp", "hfo", lambda e: e.dma_start(out=self.nst, in_=hfo[:]), reads=[g + "hfo"])
            P.barrier()
            P.emit()


_CACHE = {}


def _rope_table():
    pos = np.arange(LS)
    r = (pos // 64).astype(np.float32)
    c = (pos % 64).astype(np.float32)
    n_freq = 16
    inv = (np.float32(10000.0) ** (-np.arange(n_freq, dtype=np.float32) / np.float32(n_freq))).astype(np.float32)
    ar = r[:, None] * inv
    ac = c[:, None] * inv
    ang = np.concatenate([ar, ar, ac, ac], axis=-1).astype(np.float32)
    cos = np.cos(ang).astype(np.float32)
    sin = np.sin(ang).astype(np.float32)
    sgn = np.concatenate([-np.ones(16), np.ones(16), -np.ones(16), np.ones(16)]).astype(np.float32)
    return np.ascontiguousarray(np.concatenate([cos, sin * sgn[None, :]], axis=-1).astype(np.float32))


def _pack_inputs(inp):
    f = lambda a: np.ascontiguousarray(np.asarray(a, dtype=np.float32))
    x_prompt, x_sample = f(inp["x_prompt"]), f(inp["x_sample"])
    c, c_ctx = f(inp["c"]), f(inp["c_ctx"])
    fm = lambda v: v.reshape(-1, 128).T
    shared = {
        "w_mod": f(inp["w_mod"])[0], "ffn_w_in": f(inp["ffn_w_in"])[0], "ffn_w_out": f(inp["ffn_w_out"])[0],
        "w_in": f(inp["w_in"])[0], "w_out": f(inp["w_out"])[0],
        "gvec": np.ascontiguousarray(np.concatenate([np.tile(f(inp["q_norm"])[0], 8), np.tile(f(inp["k_norm"])[0], 2)])),
        "rope": _rope_table(),
    }
    wa, wx = f(inp["lru_wa"])[0], f(inp["lru_wx"])[0]
    bd = np.zeros((128, 16, 128), np.float32)
    for d in range(2):
        for gi, w in enumerate((wa, wx)):
            for ch in range(4):
                i = (d * 2 + gi) * 4 + ch
                bd[0:64, i, 0:64] = w[d, 2 * ch]
                bd[64:128, i, 64:128] = w[d, 2 * ch + 1]
    shared["bd"] = np.ascontiguousarray(bd.reshape(128, 16 * 128))
    pbase = np.zeros((128, NPAR), np.float32)
    pbase[:, C_BMOD:C_BMOD + 72] = fm(f(inp["b_mod"])[0])
    pbase[:, C_NPRE:C_NPRE + 24] = fm(f(inp["norm_pre"])[0].reshape(-1))
    pbase[:, C_NPOST:C_NPOST + 24] = fm(f(inp["norm_post"])[0].reshape(-1))
    cw = f(inp["conv_w"])[0]
    pbase[:, C_CONVW:C_CONVW + 16] = cw.reshape(4, 4, 128).transpose(2, 1, 0).reshape(128, 16)
    pbase[:, C_CONVB:C_CONVB + 4] = fm(f(inp["conv_b"])[0])
    pbase[:, C_BA:C_BA + 8] = fm(f(inp["lru_ba"])[0].reshape(-1))
    pbase[:, C_BX:C_BX + 8] = fm(f(inp["lru_bx"])[0].reshape(-1))
    pbase[:, C_LAM:C_LAM + 8] = fm(f(inp["lru_lambda"])[0].reshape(-1))
    cache_k, cache_v, state = f(inp["cache_k"]), f(inp["cache_v"]), f(inp["state_lru"])
    maps = []
    for core in range(NCORES):
        pr = pbase.copy()
        c2 = np.stack([fm(c[core]), fm(c_ctx)], axis=-1)
        pr[:, C_C2:C_C2 + 16] = c2.reshape(128, 16)
        pr[:, C_H0:C_H0 + 8] = fm(state[core, 0].reshape(-1))
        m = dict(shared)
        m["params"] = pr
        m["x_all"] = np.ascontiguousarray(np.concatenate([x_sample[core], x_prompt[4 * core:4 * core + 4].reshape(4 * LP, D)], axis=0))
        m["cache_k"] = np.ascontiguousarray(cache_k[core, 0].reshape(PAST, 128))
        m["cache_v"] = np.ascontiguousarray(cache_v[core, 0].reshape(PAST, 128))
        maps.append(m)
    return maps


def kernel(**inputs):
    if "nc" not in _CACHE:
        _CACHE["nc"] = Builder().build()
    nc = _CACHE["nc"]
    maps = _pack_inputs(inputs)
    res = run_bass_kernel_spmd(nc, maps, core_ids=list(range(NCORES)))
    R = res.results
    y_sample = np.stack([R[i]["y_all"][:LS] for i in range(NCORES)], axis=0)
    y_prompt = np.concatenate([R[i]["y_all"][LS:].reshape(4, LP, D) for i in range(NCORES)], axis=0)
    nck = np.concatenate([R[i]["nck"].reshape(4, 1, LP, 2, 64) for i in range(NCORES)], axis=0)
    ncv = np.concatenate([R[i]["ncv"].reshape(4, 1, LP, 2, 64) for i in range(NCORES)], axis=0)
    nst = np.concatenate([R[i]["nst"].reshape(4, 1, 2, 512) for i in range(NCORES)], axis=0)
    return (y_prompt.astype(np.float32), y_sample.astype(np.float32), nck.astype(np.float32),
            ncv.astype(np.float32), nst.astype(np.float32))
```
